# Optimizing a Trainium2 kernel written in Bass

```python
import jax, jax.numpy as jnp
from jax import lax
import numpy as np

D_MODEL = 2048
BATCH = 4
SEQ = 4096
DEPTH = 2

GRID_W = 64
CTX_LEN = 256
HEAD_DIM = 128
AXIS_DIM = HEAD_DIM // 2
ATT_Q_HEADS = 12
ATT_KV_HEADS = 4
ATT_GROUPS = ATT_Q_HEADS // ATT_KV_HEADS
ATT_WIDTH = ATT_Q_HEADS * HEAD_DIM
KV_WIDTH = ATT_KV_HEADS * HEAD_DIM
AUX_WIDTH = D_MODEL - ATT_WIDTH
AUX_GROUPS = 4
AUX_GROUP_DIM = AUX_WIDTH // AUX_GROUPS
CONV_WIDTH = 3
POOL_WINDOWS = (2, 4, 8, 16)
WINDOW = 128
Q_BLOCK = 128
BAND = Q_BLOCK + 2 * WINDOW
ROPE_THETA = 10000.0
FFN_HIDDEN = -(-8 * D_MODEL // (3 * 256)) * 256
N_MOD = 6
EPS = 1e-6
NEG_INF = -1e30
ATT_SCALE = HEAD_DIM ** -0.5
IN_WIDTHS_AB = (ATT_WIDTH, KV_WIDTH, KV_WIDTH, AUX_WIDTH, AUX_WIDTH, AUX_WIDTH)
IN_WIDTHS_CD = (ATT_WIDTH, KV_WIDTH, KV_WIDTH, AUX_WIDTH)

kernel_name = "hybrid_prefix_dit_block"


def _split_points(widths):
    return [int(v) for v in np.cumsum(widths)[:-1]]


def _rms_norm(x, g):
    xf = x.astype(jnp.float32)
    y = xf * lax.rsqrt(jnp.mean(xf * xf, axis=-1, keepdims=True) + EPS)
    return (y * g.astype(jnp.float32)).astype(x.dtype)


def _modulate(h, shift, scale):
    return h * (1 + scale) + shift


def _axial_rope(n):
    rows = n // GRID_W
    row = jnp.broadcast_to(jnp.arange(rows, dtype=jnp.float32)[:, None], (rows, GRID_W)).reshape(-1)
    col = jnp.broadcast_to(jnp.arange(GRID_W, dtype=jnp.float32)[None, :], (rows, GRID_W)).reshape(-1)
    inv = jnp.power(ROPE_THETA, -jnp.arange(0, AXIS_DIM, 2, dtype=jnp.float32) / AXIS_DIM)
    ang_r = row[:, None] * inv
    ang_c = col[:, None] * inv
    return (jnp.cos(ang_r), jnp.sin(ang_r), jnp.cos(ang_c), jnp.sin(ang_c))


def _rotate(x, cos, sin):
    x1, x2 = jnp.split(x, 2, axis=-1)
    c = cos[:, None, :].astype(x.dtype)
    s = sin[:, None, :].astype(x.dtype)
    return jnp.concatenate([x1 * c - x2 * s, x2 * c + x1 * s], axis=-1)


def _apply_axial_rope(x, rope):
    cr, sr, cc, sc = rope
    return jnp.concatenate([_rotate(x[..., :AXIS_DIM], cr, sr), _rotate(x[..., AXIS_DIM:], cc, sc)], axis=-1)


def _softmax_attend(q, k, v, bias=None, sink=None):
    s = jnp.einsum('bqkgd,bnkd->bkgqn', q, k).astype(jnp.float32) * ATT_SCALE
    if bias is not None:
        s = s + bias
    if sink is not None:
        sink_col = jnp.broadcast_to(sink.astype(jnp.float32)[None, :, :, None, None], s.shape[:-1] + (1,))
        p = jax.nn.softmax(jnp.concatenate([s, sink_col], axis=-1), axis=-1)[..., :-1]
    else:
        p = jax.nn.softmax(s, axis=-1)
    return jnp.einsum('bkgqn,bnkd->bqkgd', p.astype(v.dtype), v)


def _dense_latent_attention(q, k, v, kc, vc):
    B, S = q.shape[0], q.shape[1]
    nb = S // Q_BLOCK
    k_all = jnp.concatenate([kc, k], axis=1)
    v_all = jnp.concatenate([vc, v], axis=1)
    qb = jnp.moveaxis(q.reshape(B, nb, Q_BLOCK, ATT_KV_HEADS, ATT_GROUPS, HEAD_DIM), 1, 0)
    o = lax.map(lambda qi: _softmax_attend(qi, k_all, v_all), qb)
    return jnp.moveaxis(o, 0, 1).reshape(B, S, ATT_WIDTH)


def _window_latent_attention(q, k, v, kc, vc, sink):
    B, S = q.shape[0], q.shape[1]
    nb = S // Q_BLOCK
    pad = ((0, 0), (WINDOW, WINDOW), (0, 0), (0, 0))
    kp = jnp.pad(k, pad)
    vp = jnp.pad(v, pad)
    qb = jnp.moveaxis(q.reshape(B, nb, Q_BLOCK, ATT_KV_HEADS, ATT_GROUPS, HEAD_DIM), 1, 0)
    ctx_bias = jnp.zeros((Q_BLOCK, kc.shape[1]), jnp.float32)

    def block(args):
        qi, bi = args
        start = bi * Q_BLOCK
        kb = lax.dynamic_slice_in_dim(kp, start, BAND, axis=1)
        vb = lax.dynamic_slice_in_dim(vp, start, BAND, axis=1)
        kpos = start - WINDOW + jnp.arange(BAND, dtype=jnp.int32)
        qpos = start + jnp.arange(Q_BLOCK, dtype=jnp.int32)
        ok = (jnp.abs(kpos[None, :] - qpos[:, None]) <= WINDOW) & (kpos[None, :] >= 0) & (kpos[None, :] < S)
        band_bias = jnp.where(ok, jnp.float32(0.0), jnp.float32(NEG_INF))
        bias = jnp.concatenate([ctx_bias, band_bias], axis=-1)
        return _softmax_attend(qi, jnp.concatenate([kc, kb], axis=1), jnp.concatenate([vc, vb], axis=1), bias, sink)

    o = lax.map(block, (qb, jnp.arange(nb, dtype=jnp.int32)))
    return jnp.moveaxis(o, 0, 1).reshape(B, S, ATT_WIDTH)


def _short_conv(u, w):
    L = u.shape[1]
    half = CONV_WIDTH // 2
    up = jnp.pad(u, ((0, 0), (half, half), (0, 0)))
    out = w[0] * up[:, 0:L]
    for j in range(1, CONV_WIDTH):
        out = out + w[j] * up[:, j:j + L]
    return out


def _multiscale_pool(u, pool_w, pool_scale):
    B, L, _ = u.shape
    ug = u.reshape(B, L, AUX_GROUPS, AUX_GROUP_DIM)
    t = jnp.arange(L, dtype=jnp.int32)
    outs = []
    for g, w in enumerate(POOL_WINDOWS):
        ui = ug[:, :, g].astype(jnp.float32)
        cs = jnp.pad(jnp.cumsum(ui, axis=1), ((0, 0), (1, 0), (0, 0)))
        lo = jnp.clip(t - w // 2, 0, L)
        hi = jnp.clip(t - w // 2 + w, 0, L)
        mean = (cs[:, hi] - cs[:, lo]) / (hi - lo).astype(jnp.float32)[None, :, None]
        outs.append((mean - ui).astype(u.dtype))
    pooled = jnp.stack(outs, axis=2)
    mixed = jnp.einsum('blgc,gcd->blgd', pooled, pool_w).reshape(B, L, AUX_WIDTH)
    return mixed * pool_scale


def _heads(t, n_heads):
    return t.reshape(t.shape[0], t.shape[1], n_heads, HEAD_DIM)


def _ctx_kv(cn, w_in, k_g):
    kv = cn @ w_in[:, ATT_WIDTH:ATT_WIDTH + 2 * KV_WIDTH]
    kc, vc = jnp.split(kv, 2, axis=-1)
    return _rms_norm(_heads(kc, ATT_KV_HEADS), k_g), _heads(vc, ATT_KV_HEADS)


def _mixer_ab(xn, cn, p, rope, need_ctx):
    B, S, _ = xn.shape
    Lc = cn.shape[1]
    q, k, v, gb, gc, u = jnp.split(xn @ p['w_in'], _split_points(IN_WIDTHS_AB), axis=-1)
    q = _apply_axial_rope(_rms_norm(_heads(q, ATT_Q_HEADS), p['q_g']), rope)
    k = _apply_axial_rope(_rms_norm(_heads(k, ATT_KV_HEADS), p['k_g']), rope)
    v = _heads(v, ATT_KV_HEADS)
    kc, vc = _ctx_kv(cn, p['w_in'], p['k_g'])
    attn = _dense_latent_attention(q, k, v, kc, vc)
    conv = gb * _short_conv(gc * u, p['conv_w'])
    y = jnp.concatenate([attn, conv], axis=-1) @ p['w_out']
    yc = None
    if need_ctx:
        qc, _, _, gbc, gcc, uc = jnp.split(cn @ p['w_in'], _split_points(IN_WIDTHS_AB), axis=-1)
        qc = _rms_norm(_heads(qc, ATT_Q_HEADS), p['q_g']).reshape(B, Lc, ATT_KV_HEADS, ATT_GROUPS, HEAD_DIM)
        attn_c = _softmax_attend(qc, kc, vc).reshape(B, Lc, ATT_WIDTH)
        conv_c = gbc * _short_conv(gcc * uc, p['conv_w'])
        yc = jnp.concatenate([attn_c, conv_c], axis=-1) @ p['w_out']
    return y, yc


def _mixer_cd(xn, cn, p, rope, need_ctx):
    B, S, _ = xn.shape
    Lc = cn.shape[1]
    sink = p['sink'].reshape(ATT_KV_HEADS, ATT_GROUPS)
    q, k, v, u = jnp.split(xn @ p['w_in'], _split_points(IN_WIDTHS_CD), axis=-1)
    q = _apply_axial_rope(_rms_norm(_heads(q, ATT_Q_HEADS), p['q_g']), rope)
    k = _apply_axial_rope(_rms_norm(_heads(k, ATT_KV_HEADS), p['k_g']), rope)
    v = _heads(v, ATT_KV_HEADS)
    kc, vc = _ctx_kv(cn, p['w_in'], p['k_g'])
    attn = _window_latent_attention(q, k, v, kc, vc, sink)
    pool = _multiscale_pool(u, p['pool_w'], p['pool_scale'])
    y = jnp.concatenate([attn, pool], axis=-1) @ p['w_out']
    yc = None
    if need_ctx:
        qc, _, _, uc = jnp.split(cn @ p['w_in'], _split_points(IN_WIDTHS_CD), axis=-1)
        qc = _rms_norm(_heads(qc, ATT_Q_HEADS), p['q_g']).reshape(B, Lc, ATT_KV_HEADS, ATT_GROUPS, HEAD_DIM)
        attn_c = _softmax_attend(qc, kc, vc, None, sink).reshape(B, Lc, ATT_WIDTH)
        pool_c = _multiscale_pool(uc, p['pool_w'], p['pool_scale'])
        yc = jnp.concatenate([attn_c, pool_c], axis=-1) @ p['w_out']
    return y, yc


def _swiglu(h, w_gate, w_up, w_down):
    return (jax.nn.silu(h @ w_gate) * (h @ w_up)) @ w_down


def setup_inputs(seed: int = 0) -> dict:
    key = jax.random.key(seed)
    keys = iter(jax.random.split(key, 64))
    f32 = jnp.float32

    def nrm(shape, scale):
        return jax.random.normal(next(keys), shape, f32) * scale

    def gain(n):
        return jnp.ones((n,), f32) + nrm((n,), 0.02)

    d_in_ab = sum(IN_WIDTHS_AB)
    d_in_cd = sum(IN_WIDTHS_CD)
    inp = {}
    inp['x'] = nrm((BATCH, SEQ, D_MODEL), 1.0)
    inp['c'] = nrm((BATCH, D_MODEL), 1.0)
    inp['ctx'] = nrm((BATCH, CTX_LEN, D_MODEL), 1.0)
    inp['c_ctx'] = nrm((D_MODEL,), 1.0)
    inp['l0_norm1_g'] = gain(D_MODEL)
    inp['l0_w_mod'] = nrm((D_MODEL, N_MOD * D_MODEL), 0.5 * D_MODEL ** -0.5)
    inp['l0_b_mod'] = nrm((N_MOD * D_MODEL,), 0.02)
    inp['l0_w_in'] = nrm((D_MODEL, d_in_ab), D_MODEL ** -0.5)
    inp['l0_q_norm_g'] = gain(HEAD_DIM)
    inp['l0_k_norm_g'] = gain(HEAD_DIM)
    inp['l0_conv_w'] = nrm((CONV_WIDTH, AUX_WIDTH), CONV_WIDTH ** -0.5)
    inp['l0_w_out'] = nrm((D_MODEL, D_MODEL), D_MODEL ** -0.5)
    inp['l0_norm2_g'] = gain(D_MODEL)
    inp['l0_w_gate'] = nrm((D_MODEL, FFN_HIDDEN), D_MODEL ** -0.5)
    inp['l0_w_up'] = nrm((D_MODEL, FFN_HIDDEN), D_MODEL ** -0.5)
    inp['l0_w_down'] = nrm((FFN_HIDDEN, D_MODEL), FFN_HIDDEN ** -0.5)
    inp['l1_norm1_g'] = gain(D_MODEL)
    inp['l1_w_mod'] = nrm((D_MODEL, N_MOD * D_MODEL), 0.5 * D_MODEL ** -0.5)
    inp['l1_b_mod'] = nrm((N_MOD * D_MODEL,), 0.02)
    inp['l1_w_in'] = nrm((D_MODEL, d_in_cd), D_MODEL ** -0.5)
    inp['l1_q_norm_g'] = gain(HEAD_DIM)
    inp['l1_k_norm_g'] = gain(HEAD_DIM)
    inp['l1_sink'] = nrm((ATT_Q_HEADS,), 0.5)
    inp['l1_pool_w'] = nrm((AUX_GROUPS, AUX_GROUP_DIM, AUX_GROUP_DIM), AUX_GROUP_DIM ** -0.5)
    inp['l1_pool_scale'] = gain(AUX_WIDTH)
    inp['l1_w_out'] = nrm((D_MODEL, D_MODEL), D_MODEL ** -0.5)
    inp['l1_norm2_g'] = gain(D_MODEL)
    inp['l1_w_gate'] = nrm((D_MODEL, FFN_HIDDEN), D_MODEL ** -0.5)
    inp['l1_w_up'] = nrm((D_MODEL, FFN_HIDDEN), D_MODEL ** -0.5)
    inp['l1_w_down'] = nrm((FFN_HIDDEN, D_MODEL), FFN_HIDDEN ** -0.5)
    inp['final_norm_g'] = gain(D_MODEL)
    return inp


def reference(x, c, ctx, c_ctx,
              l0_norm1_g, l0_w_mod, l0_b_mod, l0_w_in, l0_q_norm_g, l0_k_norm_g, l0_conv_w, l0_w_out,
              l0_norm2_g, l0_w_gate, l0_w_up, l0_w_down,
              l1_norm1_g, l1_w_mod, l1_b_mod, l1_w_in, l1_q_norm_g, l1_k_norm_g, l1_sink, l1_pool_w,
              l1_pool_scale, l1_w_out, l1_norm2_g, l1_w_gate, l1_w_up, l1_w_down,
              final_norm_g):
    layers = [
        dict(norm1_g=l0_norm1_g, w_mod=l0_w_mod, b_mod=l0_b_mod, w_in=l0_w_in, q_g=l0_q_norm_g,
             k_g=l0_k_norm_g, conv_w=l0_conv_w, w_out=l0_w_out, norm2_g=l0_norm2_g,
             w_gate=l0_w_gate, w_up=l0_w_up, w_down=l0_w_down),
        dict(norm1_g=l1_norm1_g, w_mod=l1_w_mod, b_mod=l1_b_mod, w_in=l1_w_in, q_g=l1_q_norm_g,
             k_g=l1_k_norm_g, sink=l1_sink, pool_w=l1_pool_w, pool_scale=l1_pool_scale, w_out=l1_w_out,
             norm2_g=l1_norm2_g, w_gate=l1_w_gate, w_up=l1_w_up, w_down=l1_w_down),
    ]
    rope = _axial_rope(x.shape[1])
    h = x
    hc = ctx
    sc = jax.nn.silu(c)
    sc_ctx = jax.nn.silu(c_ctx)
    for i in range(DEPTH):
        p = layers[i]
        need_ctx = i < DEPTH - 1
        shift1, scale1, gate1, shift2, scale2, gate2 = jnp.split(sc @ p['w_mod'] + p['b_mod'], N_MOD, axis=-1)
        cshift1, cscale1, cgate1, cshift2, cscale2, cgate2 = jnp.split(sc_ctx @ p['w_mod'] + p['b_mod'], N_MOD, axis=-1)
        xn = _modulate(_rms_norm(h, p['norm1_g']), shift1[:, None], scale1[:, None])
        cn = _modulate(_rms_norm(hc, p['norm1_g']), cshift1, cscale1)
        if i % 2 == 0:
            y, yc = _mixer_ab(xn, cn, p, rope, need_ctx)
        else:
            y, yc = _mixer_cd(xn, cn, p, rope, need_ctx)
        h = h + gate1[:, None] * y
        hn = _modulate(_rms_norm(h, p['norm2_g']), shift2[:, None], scale2[:, None])
        h = h + gate2[:, None] * _swiglu(hn, p['w_gate'], p['w_up'], p['w_down'])
        if need_ctx:
            hc = hc + cgate1 * yc
            hcn = _modulate(_rms_norm(hc, p['norm2_g']), cshift2, cscale2)
            hc = hc + cgate2 * _swiglu(hcn, p['w_gate'], p['w_up'], p['w_down'])
    return _rms_norm(h, final_norm_g)
```

```python
import numpy as np
import concourse.bass as bass
import concourse.mybir as mybir
from concourse.bass_utils import run_bass_kernel_spmd

F32 = mybir.dt.float32
BF16 = mybir.dt.bfloat16
AF = mybir.ActivationFunctionType
ALU = mybir.AluOpType

D = 2048
KD = 16
S = 4096
LC = 256
HD = 128
NQH = 12
NKV = 4
FF = 5632
NF = 44
EPS = 1e-6
ATT_SCALE = HD ** -0.5
NLOC = 2176
NH1 = NLOC + LC
ZC = 2560 + LC

V_G1 = [0, 16 + 16 + 2 + 96]
V_G2 = [16, V_G1[1] + 16]
V_QG = [32, V_G1[1] + 32]
V_KG = [33, V_G1[1] + 33]
V_BM = [34, V_G1[1] + 34]
_o = 2 * 130
V_GF = _o
V_CONV = _o + 16
V_PSC = _o + 28
V_SINK = _o + 32
V_CVEC = _o + 44
NV = _o + 76
C_ONES = 0
C_ROT = 128
C_M3 = 256
C_BM = 640
C_PW = C_BM + 2048
NCB = C_PW + 512


class Slot:
    __slots__ = ("name", "w", "rs", "const")

    def __init__(self, name):
        self.name = name
        self.w = None
        self.rs = {}
        self.const = False


class Sched:
    def __init__(self, nc):
        self.nc = nc
        self.q = {e: [] for e in ("pe", "act", "dve", "pool", "sp")}
        self.esem = {e: nc.alloc_semaphore("es_" + e) for e in ("pe", "act", "dve")}
        self.ecnt = {e: 0 for e in self.esem}
        self.waited = {e: {} for e in self.q}
        self.dsem = {}
        self.nops = 0
        self.plan = False

    def _wait(self, eng, tok):
        if tok is None:
            return
        sem, val = tok
        if eng == "pe" and sem is self.esem["pe"]:
            return
        k = id(sem)
        if self.waited[eng].get(k, 0) >= val:
            return
        self.waited[eng][k] = val
        self.q[eng].append(("wait", sem, val))

    def _deps(self, eng, reads, writes):
        for s in reads:
            self._wait(eng, s.w)
        for s in writes:
            self._wait(eng, s.w)
            for sem, val in s.rs.values():
                self._wait(eng, (sem, val))

    def _mark(self, tok, reads, writes):
        sem, val = tok
        for s in reads:
            if not s.const:
                s.rs[id(sem)] = (sem, val)
        for s in writes:
            s.w = tok
            s.rs = {}

    def op(self, eng, fn, reads=(), writes=()):
        if self.plan:
            return None
        self._deps(eng, reads, writes)
        self.ecnt[eng] += 1
        tok = (self.esem[eng], self.ecnt[eng])
        self.q[eng].append(("op", fn, self.esem[eng]))
        self._mark(tok, reads, writes)
        self.nops += 1
        return tok

    def dma(self, q, out, in_, key, reads=(), writes=(), chain=False):
        if self.plan:
            return None
        if key not in self.dsem:
            self.dsem[key] = [self.nc.alloc_semaphore("ds_" + key), 0]
        ent = self.dsem[key]
        if ent[1] > 0 and not chain:
            self._wait(q, (ent[0], ent[1]))
        self._deps(q, reads, writes)
        ent[1] += 16
        tok = (ent[0], ent[1])
        self.q[q].append(("dma", out, in_, ent[0]))
        self._mark(tok, reads, writes)
        return tok

    def barrier(self, engs=("pe", "act", "dve", "sp"), skip_keys=("ring0", "ring1", "ring2", "ring3")):
        if self.plan:
            return
        for e in engs:
            for o in ("pe", "act", "dve"):
                if self.ecnt[o] > 0:
                    self._wait(e, (self.esem[o], self.ecnt[o]))
            for k, ent in self.dsem.items():
                if k in skip_keys or ent[1] == 0:
                    continue
                self._wait(e, (ent[0], ent[1]))

    def emit(self):
        nc = self.nc
        qs = self.q

        def run(lst, e):
            for it in lst:
                if it[0] == "wait":
                    e.wait_ge(it[1], it[2])
                elif it[0] == "op":
                    it[1](e).then_inc(it[2], 1)
                else:
                    e.dma_start(out=it[1], in_=it[2]).then_inc(it[3], 16)

        with nc.Block() as block:
            @block.tensor
            def _(e):
                run(qs["pe"], e)

            @block.scalar
            def _(e):
                run(qs["act"], e)

            @block.vector
            def _(e):
                run(qs["dve"], e)

            @block.gpsimd
            def _(e):
                run(qs["pool"], e)

            @block.sync
            def _(e):
                run(qs["sp"], e)


def build_program(debug=False, stop=None):
    nc = bass.Bass("TRN2", target_bir_lowering=False)
    sc = Sched(nc)

    def din(name, shape, dt=F32):
        return nc.dram_tensor(name, shape, dt, kind="ExternalInput").ap()

    xT = din("xT", [D, S + LC])
    rope = din("rope", [2, 128, S])
    vecs_d = din("vecs", [128, NV])
    cbf_d = din("cbf", [128, NCB])
    W = []
    for l in range(2):
        W.append(dict(
            mod=din(f"w_mod{l}", [D, 6 * D]),
            win=din(f"w_in{l}", [D, 4096 if l == 0 else 3072]),
            wout=din(f"w_out{l}", [D, D]),
            wg=din(f"w_gate{l}", [D, FF]),
            wu=din(f"w_up{l}", [D, FF]),
            wd=din(f"w_down{l}", [FF, D]),
        ))
    outT = nc.dram_tensor("outT", [D, 2048], F32, kind="ExternalOutput").ap()
    skind = "ExternalOutput" if debug else "Internal"

    def dscr(name, shape, dt):
        return nc.dram_tensor(name, shape, dt, kind=skind).ap()

    kT_d = dscr("kT_d", [NKV, 128, S + LC], BF16)
    v_d = dscr("v_d", [NKV, 128, 34, 128], BF16)
    z_d = dscr("z_d", [4, 128, ZC], F32)
    gb_d = dscr("gb_d", [4, 128, ZC], F32)
    u_d = dscr("u_d", [128, 17, 512], BF16)
    hT_d = dscr("hT_d", [KD, 128, NH1], F32)

    ARENA = 53100
    arena = nc.alloc_sbuf_tensor("arena", [128, ARENA], F32)
    arena_ap = arena[:, :]
    ptr = [0]

    def carve(words, at=None):
        if at is None:
            at = ptr[0]
            ptr[0] += words
        assert at + words <= ARENA, (at, words)
        return arena[:, at:at + words]

    def bf(ap, **kw):
        v = ap.bitcast(BF16)
        if kw:
            v = v.rearrange("p (a b) -> p a b", **kw)
        return v

    def f3(ap, **kw):
        return ap.rearrange("p (a b) -> p a b", **kw)

    cb = bf(carve(NCB // 2))
    vecs = carve(NV)
    dv = carve(512)
    hn = bf(carve(KD * 1024 // 2), a=KD)
    NRING = 3
    ring = [bf(carve(4096)) for _ in range(NRING)]
    base = ptr[0]
    Xb = f3(carve(KD * 512), a=KD)
    xn = bf(carve(KD * 512 // 2), a=KD)
    qa = bf(carve(KD * 512 // 2), a=KD)
    kvr = [bf(carve(2176)) for _ in range(2)]
    Pb = [bf(carve(512)) for _ in range(3)]
    tmp = [carve(512) for _ in range(6)]
    cosb = carve(512)
    sinb = carve(512)
    sqb = [bf(carve(256)) for _ in range(2)]
    knb = [bf(carve(256)) for _ in range(2)]
    end1 = ptr[0]
    xnA = bf(arena[:, base + KD * 512:base + KD * 512 + KD * 512], a=KD)
    ptr[0] = base + KD * 512 + KD * 512
    gcS = f3(carve(4 * 1024), a=4)
    zbuf = [carve(512) for _ in range(3)]
    assert ptr[0] <= base + KD * 512 + KD * 512 + 2 * 2176 + 3 * 512
    ptr[0] = end1
    zh = f3(carve(4 * 516), a=4)
    end1b = ptr[0]
    Psum = [bf(carve(256)) for _ in range(2)]
    end1c = ptr[0]
    ptr[0] = end1
    vbuf = [bf(carve(256)) for _ in range(2)]
    kout = [bf(carve(256)) for _ in range(2)]
    cos2 = carve(512)
    sin2 = carve(512)
    assert ptr[0] <= end1b
    ptr[0] = end1
    ub = bf(carve(6 * 256), a=6)
    pls = [bf(carve(256)) for _ in range(2)]
    assert ptr[0] <= end1b
    ptr[0] = base
    aT = bf(carve(22 * 512), a=22)
    sil = [carve(512) for _ in range(4)]
    hb = [carve(1024) for _ in range(3)]
    assert ptr[0] <= ARENA and end1c <= ARENA, (ptr[0], end1c)

    pairs = [nc.alloc_psum_tensor(f"pair{i}", [128, 1024], F32) for i in range(4)]
    banks = [pairs[i // 2][:, (i % 2) * 512:(i % 2 + 1) * 512] for i in range(8)]

    def mk(n):
        return Slot(n)

    s_cb, s_vecs, s_dv, s_hn = mk("cb"), mk("vecs"), mk("dv"), mk("hn")
    s_ring = [mk(f"ring{i}") for i in range(NRING)]
    s_Xb, s_xn = [mk(f"Xb{c}") for c in range(16)], mk("xn")
    s_qa = [mk(f"qa{i}") for i in range(16)]
    s_kvr = [mk("kvr0"), mk("kvr1")]
    s_P = [mk(f"P{i}") for i in range(3)]
    s_Psum = [mk("Ps0"), mk("Ps1")]
    s_tmp = [mk(f"tmp{i}") for i in range(6)]
    s_cos, s_sin = mk("cos"), mk("sin")
    s_cos2, s_sin2 = mk("cos2"), mk("sin2")
    s_xnB = mk("xnB")
    s_sq = [mk("sq0"), mk("sq1")]
    s_knb = [mk("knb0"), mk("knb1")]
    s_vbuf = [mk("vb0"), mk("vb1")]
    s_kout = [mk("ko0"), mk("ko1")]
    s_gcS = mk("gcS")
    s_zbuf = [mk(f"zb{i}") for i in range(3)]
    s_zh, s_gbb, s_ub = mk("zh"), mk("gbb"), mk("ub")
    s_pls = [mk("pls0"), mk("pls1")]
    s_aT = [mk(f"aT{i}") for i in range(22)]
    s_sil = [mk(f"sil{i}") for i in range(4)]
    s_hb = [mk(f"hb{i}") for i in range(3)]
    s_bank = [mk(f"bank{i}") for i in range(8)]
    s_dram = {}

    def sd(name):
        if name not in s_dram:
            s_dram[name] = mk("d_" + name)
        return s_dram[name]

    ring_state = {"next": 0}

    def wload(parts):
        i = ring_state["next"] % NRING
        ring_state["next"] += 1
        first = True
        for dstf, src in parts:
            sc.dma("pool", dstf(ring[i]), src, key=f"ring{i}", writes=[s_ring[i]], chain=not first)
            first = False
        return i

    class WStream:
        def __init__(self):
            self.items = []
            self.loaded = 0
            self.used = 0
            self.slot_of = {}
            self.addi = 0

        def add(self, parts):
            if sc.plan:
                self.items.append(parts)
                return len(self.items) - 1
            self.addi += 1
            return self.addi - 1

        def pump(self):
            while self.loaded < len(self.items) and self.loaded < self.used + NRING:
                self.slot_of[self.loaded] = wload(self.items[self.loaded])
                self.loaded += 1

        def take(self):
            if sc.plan:
                return 0
            self.pump()
            assert self.used < self.loaded
            i = self.slot_of[self.used]
            self.used += 1
            return i

        def done(self):
            if not sc.plan:
                self.pump()

    ws = WStream()

    def k16(slot_ap, ncols=512):
        return slot_ap[:, 0:16 * ncols].rearrange("p (k n) -> p k n", k=16)

    def wsrc16(w, c0, ncols=512):
        return w[:, c0:c0 + ncols].rearrange("(k p) n -> p k n", p=128)

    def add_w16(w, c0, ncols=512):
        return ws.add([(lambda r, ncols=ncols: k16(r, ncols), wsrc16(w, c0, ncols))])

    ones = cb[:, C_ONES:C_ONES + 128]
    rotm = cb[:, C_ROT:C_ROT + 128]

    def vcol(c, n=1):
        return vecs[:, c:c + n]

    def dcol(c, n=1):
        return dv[:, c:c + n]

    DV_SCV = 0
    DV_MOD = 16
    DV_QG = 400
    DV_GF = 404
    DV_SINKE = 420
    DV_CD = 432
    DV_CH = 433

    def modc(l, j, m, c=0):
        return DV_MOD + (l * 2 + j) * 96 + m * 16 + c

    scv = bf(dv[:, DV_SCV:DV_SCV + 16])

    sc.dma("pool", cb[:, :], cbf_d[:, :], key="cb", writes=[s_cb])
    sc.dma("sp", vecs[:, :], vecs_d[:, :], key="vecs", writes=[s_vecs])
    s_cb.const = True
    sc.op("act", lambda e: e.activation(out=scv[:, :], in_=vcol(V_CVEC, 32), func=AF.Silu),
          reads=[s_vecs], writes=[s_dv])
    sc.op("act", lambda e: e.activation(out=dcol(DV_SINKE, 12), in_=vcol(V_SINK, 12), func=AF.Exp),
          reads=[s_vecs], writes=[s_dv])
    sc.op("dve", lambda e: e.memset(dcol(DV_CD), float(D * EPS)), writes=[s_dv])
    sc.op("dve", lambda e: e.memset(dcol(DV_CH), float(HD * EPS)), writes=[s_dv])
    for l in range(2):
        sc.op("dve", lambda e, l=l: e.tensor_scalar(out=dcol(DV_QG + 2 * l, 2), in0=vcol(V_QG[l], 2),
                                                    scalar1=float(np.sqrt(HD)), scalar2=None, op0=ALU.mult),
              reads=[s_vecs], writes=[s_dv])
    sc.op("dve", lambda e: e.tensor_scalar(out=dcol(DV_GF, 16), in0=vcol(V_GF, 16),
                                           scalar1=float(np.sqrt(D)), scalar2=None, op0=ALU.mult),
          reads=[s_vecs], writes=[s_dv])

    def emit_mod_tile(l, p, t, bank_i=0):
        wm = W[l]["mod"]
        add_w16(wm, (4 * p + t) * 512)
        mb = banks[bank_i]
        ri = ws.take()
        wt = k16(ring[ri])

        def fn(e):
            ins = None
            for cc in range(4):
                for k in range(16):
                    ins = e.matmul(mb[:, 2 * cc:2 * cc + 2], lhsT=wt[:, k, cc * 128:(cc + 1) * 128],
                                   rhs=scv[:, 2 * k:2 * k + 2], start=(k == 0), stop=(k == 15))
            return ins
        sc.op("pe", fn, reads=[s_ring[ri], s_dv], writes=[s_bank[bank_i]])
        ws.done()
        mv = mb[:, 0:8].rearrange("p (c j) -> p c j", j=2)
        for j in range(2):
            sc.op("dve", lambda e, j=j: e.tensor_tensor(out=dcol(modc(l, j, p, 4 * t), 4), in0=mv[:, :, j],
                                                        in1=vcol(V_BM[l] + 16 * p + 4 * t, 4), op=ALU.add),
                  reads=[s_bank[bank_i], s_vecs], writes=[s_dv])

    def emit_mod_fin(l, p):
        if p not in (1, 4):
            return
        gc = V_G1[l] if p == 1 else V_G2[l]
        for j in range(2):
            sc.op("dve", lambda e, j=j: e.tensor_scalar(
                out=dcol(modc(l, j, p), 16), in0=dcol(modc(l, j, p), 16), scalar1=1.0,
                scalar2=float(np.sqrt(D)), op0=ALU.add, op1=ALU.mult), reads=[s_dv], writes=[s_dv])
            sc.op("dve", lambda e, j=j, gc=gc: e.tensor_tensor(
                out=dcol(modc(l, j, p), 16), in0=dcol(modc(l, j, p), 16), in1=vcol(gc, 16),
                op=ALU.mult), reads=[s_dv, s_vecs], writes=[s_dv])

    def emit_mod_part(l, p, bank_i=0):
        for t in range(4):
            emit_mod_tile(l, p, t, bank_i)
        emit_mod_fin(l, p)

    def load_Xb(src3, n, extra_reads=()):
        for g4 in range(4):
            sc.dma("sp", Xb[:, 4 * g4:4 * g4 + 4, 0:n], src3[:, 4 * g4:4 * g4 + 4, :], key=f"Xb{g4}",
                   reads=list(extra_reads), writes=s_Xb[4 * g4:4 * g4 + 4])

    def store_Xb(dst3, n, dslot):
        for g4 in range(4):
            sc.dma("sp", dst3[:, 4 * g4:4 * g4 + 4, :], Xb[:, 4 * g4:4 * g4 + 4, 0:n], key=f"Xb{g4}",
                   reads=s_Xb[4 * g4:4 * g4 + 4], writes=[dslot])

    def load_rope(col0, n):
        sc.dma("sp", cosb[:, 0:n], rope[0, :, col0:col0 + n], key="cos", writes=[s_cos])
        sc.dma("sp", sinb[:, 0:n], rope[1, :, col0:col0 + n], key="sin", writes=[s_sin])

    def sq_chunk(c, n):
        sq_i = c % 2
        if sq_i == 0:
            sc.op("act", lambda e: e.activation(out=sqb[0][:, 0:n], in_=Xb[:, c, 0:n], func=AF.Square),
                  reads=[s_Xb[c]], writes=[s_sq[0]])
        else:
            sc.op("dve", lambda e: e.tensor_tensor(out=sqb[1][:, 0:n], in0=Xb[:, c, 0:n], in1=Xb[:, c, 0:n], op=ALU.mult),
                  reads=[s_Xb[c]], writes=[s_sq[1]])
        sc.op("pe", lambda e: e.matmul(banks[0][:, 0:n], lhsT=ones, rhs=sqb[sq_i][:, 0:n],
                                       start=(c == 0), stop=(c == KD - 1)),
              reads=[s_sq[sq_i], s_cb], writes=[s_bank[0]])

    def rstd_finish(n):
        sc.op("act", lambda e: e.activation(out=tmp[0][:, 0:n], in_=banks[0][:, 0:n], func=AF.Ln, bias=dcol(DV_CD), scale=1.0),
              reads=[s_bank[0], s_dv], writes=[s_tmp[0]])
        sc.op("act", lambda e: e.activation(out=tmp[0][:, 0:n], in_=tmp[0][:, 0:n], func=AF.Exp, scale=-0.5),
              reads=[s_tmp[0]], writes=[s_tmp[0]])

    def norm_finish(n, acol, bcol, dst, dst_slot, dst_c0=0):
        rstd_finish(n)
        for c in range(KD):
            ti = 1 + (c % 2)
            sc.op("dve", lambda e, c=c, ti=ti: e.scalar_tensor_tensor(
                out=tmp[ti][:, 0:n], in0=Xb[:, c, 0:n], scalar=dcol(acol + c), in1=tmp[0][:, 0:n],
                op0=ALU.mult, op1=ALU.mult), reads=[s_Xb[c], s_dv, s_tmp[0]], writes=[s_tmp[ti]])
            sc.op("act", lambda e, c=c, ti=ti: e.activation(
                out=dst[:, c, dst_c0:dst_c0 + n], in_=tmp[ti][:, 0:n], func=AF.Identity,
                bias=dcol(bcol + c), scale=1.0), reads=[s_tmp[ti], s_dv], writes=[dst_slot])

    def norm_mod(n, acol, bcol, dst, dst_slot, dst_c0=0):
        for c in range(KD):
            sq_chunk(c, n)
        norm_finish(n, acol, bcol, dst, dst_slot, dst_c0)

    def proj_fm(ri, cc, n, bank_i, src=None, src_slot=None, c0=0):
        src = xn if src is None else src
        src_slot = s_xn if src_slot is None else src_slot
        wt = k16(ring[ri])
        bk = banks[bank_i]

        def fn(e):
            ins = None
            for k in range(KD):
                ins = e.matmul(bk[:, 0:n], lhsT=wt[:, k, cc * 128:(cc + 1) * 128], rhs=src[:, k, c0:c0 + n],
                               start=(k == 0), stop=(k == KD - 1))
            return ins
        sc.op("pe", fn, reads=[s_ring[ri], src_slot], writes=[s_bank[bank_i]])

    def proj_heads(items, n_default=None):
        PB, SB, RB = (1, 2, 4), (0, 3), (5, 6)
        cur = {"ri": None}
        ni = len(items)

        def proj(i):
            it = items[i]
            if it.get("take", it["cc"] == 0):
                cur["ri"] = ws.take()
            proj_fm(cur["ri"], it["cc"], it.get("n", n_default), PB[i % 3], src=it.get("src"),
                    src_slot=it.get("src_slot"), c0=it.get("c0", 0))
            if it["last"]:
                ws.done()

        def A(i):
            it = items[i]
            n = it.get("n", n_default)
            p = i % 2
            pb, sb_ = PB[i % 3], SB[p]
            tr, tk = (3, 0)[p], (4, 1)[p]
            bk = banks[pb]
            sc.op("act", lambda e: e.activation(out=sqb[p][:, 0:n], in_=bk[:, 0:n], func=AF.Square),
                  reads=[s_bank[pb]], writes=[s_sq[p]])
            sc.op("pe", lambda e: e.matmul(banks[sb_][:, 0:n], lhsT=ones, rhs=sqb[p][:, 0:n], start=True, stop=True),
                  reads=[s_sq[p], s_cb], writes=[s_bank[sb_]])
            sc.op("act", lambda e: e.activation(out=tmp[tr][:, 0:n], in_=banks[sb_][:, 0:n], func=AF.Ln,
                                                bias=dcol(DV_CH), scale=1.0),
                  reads=[s_bank[sb_], s_dv], writes=[s_tmp[tr]])
            sc.op("act", lambda e: e.activation(out=tmp[tr][:, 0:n], in_=tmp[tr][:, 0:n], func=AF.Exp, scale=-0.5),
                  reads=[s_tmp[tr]], writes=[s_tmp[tr]])
            if not it["rope"]:
                sc.op("dve", lambda e: e.scalar_tensor_tensor(
                    out=it["dst"], in0=bk[:, 0:n], scalar=dcol(it["gcol"]), in1=tmp[tr][:, 0:n],
                    op0=ALU.mult, op1=ALU.mult), reads=[s_bank[pb], s_dv, s_tmp[tr]], writes=it["slots"])
                return
            sc.op("dve", lambda e: e.scalar_tensor_tensor(
                out=tmp[tk][:, 0:n], in0=bk[:, 0:n], scalar=dcol(it["gcol"]), in1=tmp[tr][:, 0:n],
                op0=ALU.mult, op1=ALU.mult), reads=[s_bank[pb], s_dv, s_tmp[tr]], writes=[s_tmp[tk]])
            sc.op("act", lambda e: e.activation(out=knb[p][:, 0:n], in_=tmp[tk][:, 0:n], func=AF.Copy),
                  reads=[s_tmp[tk]], writes=[s_knb[p]])

        def B(i):
            it = items[i]
            n = it.get("n", n_default)
            cosb_, sinb_, s_cos_, s_sin_ = it.get("tables", (cosb, sinb, s_cos, s_sin))
            p = i % 2
            tk, t5, rb = (4, 1)[p], (5, 2)[p], RB[p]
            if it["rope"]:
                sc.op("pe", lambda e: e.matmul(banks[rb][:, 0:n], lhsT=rotm, rhs=knb[p][:, 0:n], start=True, stop=True),
                      reads=[s_knb[p], s_cb], writes=[s_bank[rb]])
                sc.op("dve", lambda e: e.tensor_tensor(out=tmp[t5][:, 0:n], in0=banks[rb][:, 0:n], in1=sinb_[:, 0:n], op=ALU.mult),
                      reads=[s_bank[rb], s_sin_], writes=[s_tmp[t5]])
                sc.op("dve", lambda e: e.tensor_tensor(out=tmp[tk][:, 0:n], in0=tmp[tk][:, 0:n], in1=cosb_[:, 0:n], op=ALU.mult),
                      reads=[s_tmp[tk], s_cos_], writes=[s_tmp[tk]])
                sc.op("dve", lambda e: e.tensor_tensor(out=it["dst"], in0=tmp[tk][:, 0:n], in1=tmp[t5][:, 0:n], op=ALU.add),
                      reads=[s_tmp[tk], s_tmp[t5]], writes=it["slots"])
            if it.get("post"):
                it["post"]()

        proj(0)
        if ni > 1:
            proj(1)
        for i in range(ni):
            A(i)
            if i + 2 < ni:
                proj(i + 2)
            B(i)

    def phase_a(l, sblocks):
        win = W[l]["win"]
        for halves in sblocks:
            nh = len(halves)
            xs = [s_xn, s_xnB]
            tabs = [(cosb, sinb, s_cos, s_sin), (cos2, sin2, s_cos2, s_sin2)]
            for hi, blk in enumerate(halves):
                n, scol = blk["n"], blk["scol"]
                jj = 1 if blk["ctx"] else 0
                src = (xT[:, scol:scol + n].rearrange("(k p) n -> p k n", p=128) if l == 0
                       else hT_d[:, :, scol:scol + n].rearrange("k p n -> p k n"))
                load_Xb(src, n, [sd("hT")] if l == 1 else [])
                if not blk["ctx"]:
                    cb_, sb_, sc_, ss_ = tabs[hi]
                    sc.dma("sp", cb_[:, 0:n], rope[0, :, blk["pos"]:blk["pos"] + n], key=f"cos{hi}", writes=[sc_])
                    sc.dma("sp", sb_[:, 0:n], rope[1, :, blk["pos"]:blk["pos"] + n], key=f"sin{hi}", writes=[ss_])
                norm_mod(n, modc(l, jj, 1), modc(l, jj, 0), xnA, xs[hi], dst_c0=512 * hi)
            add_w16(win, 1536)
            kitems = []
            ix = 0
            for h in range(NKV):
                for hi, blk in enumerate(halves):
                    ko = ix % 2
                    ix += 1

                    def post(h=h, ko=ko, n=blk["n"], kcol=blk["kcol"]):
                        sc.dma("sp", kT_d[h, :, kcol:kcol + n], kout[ko][:, 0:n], key=f"ko{ko}",
                               reads=[s_kout[ko]], writes=[sd("kT")])
                    kitems.append(dict(cc=h, take=(h == 0 and hi == 0), last=(h == NKV - 1 and hi == nh - 1),
                                       gcol=DV_QG + 2 * l + 1, rope=not blk["ctx"], n=blk["n"], c0=512 * hi,
                                       src=xnA, src_slot=xs[hi], tables=tabs[hi],
                                       dst=kout[ko][:, 0:blk["n"]], slots=[s_kout[ko]], post=post))
            proj_heads(kitems)

            def tok_major(wcol, which):
                add_w16(win, wcol)
                ri = ws.take()
                wt = k16(ring[ri])
                ix = 0
                for hi, blk in enumerate(halves):
                    if which == "u" and not blk["aux"]:
                        continue
                    for tt in range(blk["n"] // 128):
                        bi_ = 4 + (ix % 2)
                        vi = ix % 2
                        ix += 1
                        c0 = 512 * hi + tt * 128

                        def fn(e, c0=c0, bi_=bi_, wt=wt):
                            ins = None
                            for k in range(KD):
                                ins = e.matmul(banks[bi_][:, 0:512], lhsT=xnA[:, k, c0:c0 + 128],
                                               rhs=wt[:, k, :], start=(k == 0), stop=(k == KD - 1))
                            return ins
                        sc.op("pe", fn, reads=[s_ring[ri], xs[hi]], writes=[s_bank[bi_]])
                        sc.op("act", lambda e, bi_=bi_, vi=vi: e.activation(out=vbuf[vi][:, :], in_=banks[bi_][:, 0:512], func=AF.Copy),
                              reads=[s_bank[bi_]], writes=[s_vbuf[vi]])
                        if which == "v":
                            tg = blk["kcol"] // 128 + tt
                            sc.dma("sp", v_d[:, :, tg, :].rearrange("h p d -> p h d"),
                                   vbuf[vi][:, :].rearrange("p (h d) -> p h d", h=4), key=f"vb{vi}",
                                   reads=[s_vbuf[vi]], writes=[sd("v")])
                        else:
                            tg = blk["scol"] // 128 + tt
                            sc.dma("sp", u_d[:, tg, :], vbuf[vi][:, :], key=f"vb{vi}", reads=[s_vbuf[vi]], writes=[sd("u")])
                ws.done()
            tok_major(2048, "v")
            if l == 1 and any(b["aux"] for b in halves):
                tok_major(2560, "u")
            if l == 0 and any(b["aux"] for b in halves):
                for which, wcol in (("gb", 2560), ("gc", 3072), ("u", 3584)):
                    add_w16(win, wcol)
                    ri = ws.take()
                    ix = 0
                    for c in range(4):
                        for hi, blk in enumerate(halves):
                            if not blk["aux"]:
                                continue
                            n, zcol, off = blk["n"], blk["zcol"], 512 * hi
                            bi_ = 6 + (ix % 2)
                            zi = ix % 3
                            ix += 1
                            proj_fm(ri, c, n, bi_, src=xnA, src_slot=xs[hi], c0=off)
                            if which == "gc":
                                sc.op("act", lambda e, bi_=bi_, c=c, n=n, off=off: e.activation(
                                    out=gcS[:, c, off:off + n], in_=banks[bi_][:, 0:n], func=AF.Copy),
                                    reads=[s_bank[bi_]], writes=[s_gcS])
                                continue
                            if which == "gb":
                                sc.op("act", lambda e, bi_=bi_, zi=zi, n=n: e.activation(
                                    out=zbuf[zi][:, 0:n], in_=banks[bi_][:, 0:n], func=AF.Copy),
                                    reads=[s_bank[bi_]], writes=[s_zbuf[zi]])
                                dst_d, dsl = gb_d, sd("gb")
                            else:
                                sc.op("dve", lambda e, bi_=bi_, zi=zi, c=c, n=n, off=off: e.tensor_tensor(
                                    out=zbuf[zi][:, 0:n], in0=banks[bi_][:, 0:n], in1=gcS[:, c, off:off + n], op=ALU.mult),
                                    reads=[s_bank[bi_], s_gcS], writes=[s_zbuf[zi]])
                                dst_d, dsl = z_d, sd("z")
                            sc.dma("sp", dst_d[c, :, zcol:zcol + n], zbuf[zi][:, 0:n], key=f"zb{zi}",
                                   reads=[s_zbuf[zi]], writes=[dsl])
                    ws.done()

    def attn_head(l, h, n, key_steps):
        O_b, D_b = 6, 7
        NSB = 3
        flat = []
        for (ksl, tiles) in key_steps:
            for t in tiles:
                flat.append((ksl,) + t)
        groups = [flat[i:i + 2] for i in range(0, len(flat), 2)]
        ng = len(groups)
        nt = len(flat)

        def emit_S(gi):
            ss_ = gi % NSB
            for ti, (ksl, k_ap, v_ap, c0, c1, m) in enumerate(groups[gi]):
                bi_ = 2 * ss_ + ti
                sc.op("pe", lambda e, bi_=bi_, k_ap=k_ap: e.matmul(
                    banks[bi_][:, 0:n], lhsT=k_ap, rhs=qa[:, h, 0:n], start=True, stop=True),
                    reads=[s_kvr[ksl], s_qa[h]], writes=[s_bank[bi_]])

        for g0 in range(min(NSB, ng)):
            emit_S(g0)
        done = 0
        for gi in range(ng):
            grp = groups[gi]
            sp = gi % 3
            ss_ = gi % NSB
            P3 = Pb[sp].rearrange("p (b n) -> p b n", b=2)
            S3 = pairs[ss_][:, :].rearrange("p (b n) -> p b n", b=2)
            nb_ = len(grp)
            sc.op("act", lambda e, P3=P3, S3=S3, nb_=nb_: e.activation(
                out=P3[:, 0:nb_, 0:n], in_=S3[:, 0:nb_, 0:n], func=AF.Exp, scale=float(ATT_SCALE)),
                reads=[s_bank[2 * ss_ + t_] for t_ in range(nb_)], writes=[s_P[sp]])
            for ti, (ksl, k_ap, v_ap, c0, c1, m) in enumerate(grp):
                if m is not None:
                    sc.op("dve", lambda e, ti=ti, c0=c0, c1=c1, m=m, P3=P3: e.tensor_tensor(
                        out=P3[:, ti, c0:c1], in0=P3[:, ti, c0:c1], in1=m, op=ALU.mult),
                        reads=[s_P[sp], s_cb], writes=[s_P[sp]])
            pair_sum = (nb_ == 2 and all(t[5] is None and t[3] == 0 and t[4] == n for t in grp))
            pq = gi % 2
            if pair_sum:
                sc.op("dve", lambda e, P3=P3, pq=pq: e.tensor_tensor(out=Psum[pq][:, 0:n], in0=P3[:, 0, 0:n], in1=P3[:, 1, 0:n], op=ALU.add),
                      reads=[s_P[sp]], writes=[s_Psum[pq]])
            for ti, (ksl, k_ap, v_ap, c0, c1, m) in enumerate(grp):
                first, last = (done == 0), (done == nt - 1)
                done += 1
                sc.op("pe", lambda e, v_ap=v_ap, ti=ti, c0=c0, c1=c1, first=first, last=last, P3=P3: e.matmul(
                    banks[O_b][:, c0:c1], lhsT=v_ap, rhs=P3[:, ti, c0:c1], start=first, stop=last),
                    reads=[s_P[sp], s_kvr[ksl]], writes=[s_bank[O_b]])
                if not pair_sum:
                    sc.op("pe", lambda e, ti=ti, c0=c0, c1=c1, first=first, last=last, P3=P3: e.matmul(
                        banks[D_b][:, c0:c1], lhsT=ones, rhs=P3[:, ti, c0:c1], start=first, stop=last),
                        reads=[s_P[sp], s_cb], writes=[s_bank[D_b]])
            if pair_sum:
                dfirst, dlast = (done == 2), (done == nt)
                sc.op("pe", lambda e, pq=pq, dfirst=dfirst, dlast=dlast: e.matmul(
                    banks[D_b][:, 0:n], lhsT=ones, rhs=Psum[pq][:, 0:n], start=dfirst, stop=dlast),
                    reads=[s_Psum[pq], s_cb], writes=[s_bank[D_b]])
            if gi + NSB < ng:
                emit_S(gi + NSB)
        sc.op("dve", lambda e: e.tensor_copy(out=tmp[2][:, 0:n], in_=banks[O_b][:, 0:n]),
              reads=[s_bank[O_b]], writes=[s_tmp[2]])
        if l == 1:
            sc.op("act", lambda e: e.activation(out=tmp[1][:, 0:n], in_=banks[D_b][:, 0:n], func=AF.Ln,
                                                bias=dcol(DV_SINKE + h), scale=1.0),
                  reads=[s_bank[D_b], s_dv], writes=[s_tmp[1]])
        else:
            sc.op("act", lambda e: e.activation(out=tmp[1][:, 0:n], in_=banks[D_b][:, 0:n], func=AF.Ln),
                  reads=[s_bank[D_b]], writes=[s_tmp[1]])
        sc.op("act", lambda e: e.activation(out=tmp[1][:, 0:n], in_=tmp[1][:, 0:n], func=AF.Exp, scale=-1.0),
              reads=[s_tmp[1]], writes=[s_tmp[1]])
        sc.op("dve", lambda e: e.tensor_tensor(out=qa[:, h, 0:n], in0=tmp[2][:, 0:n], in1=tmp[1][:, 0:n], op=ALU.mult),
              reads=[s_tmp[2], s_tmp[1]], writes=[s_qa[h]])

    kv_state = {"i": 0}

    def kv_load(g, kranges, vranges):
        i = kv_state["i"] % 2
        kv_state["i"] += 1
        first = True
        for (dc, scol, ncol) in kranges:
            sc.dma("sp", kvr[i][:, dc:dc + ncol], kT_d[g, :, scol:scol + ncol], key=f"kvr{i}",
                   reads=[sd("kT")], writes=[s_kvr[i]], chain=not first)
            first = False
        for (dt_, st, ntl) in vranges:
            dstv = kvr[i][:, 2176 + dt_ * 128:2176 + (dt_ + ntl) * 128].rearrange("p (t d) -> p t d", d=128)
            sc.dma("sp", dstv, v_d[g, :, st:st + ntl, :], key=f"kvr{i}", reads=[sd("v")], writes=[s_kvr[i]], chain=True)
        return i

    def ktile(i, t):
        return kvr[i][:, t * 128:(t + 1) * 128]

    def vtile(i, t):
        return kvr[i][:, 2176 + t * 128:2176 + (t + 1) * 128]

    def stage1(l, sb, hn_c0, mod_part=None):
        n, kind = sb["n"], sb["kind"]
        j = 1 if kind == "ctx" else 0
        use_rope = kind != "ctx"
        win, wout = W[l]["win"], W[l]["wout"]
        tq = [add_w16(win, 512 * t) for t in range(3)]
        scol, hcol = sb["scol"], sb["hcol"]
        src = (xT[:, scol:scol + n].rearrange("(k p) n -> p k n", p=128) if l == 0
               else hT_d[:, :, scol:scol + n].rearrange("k p n -> p k n"))
        load_Xb(src, n, [sd("hT")] if l == 1 else [])
        if use_rope:
            load_rope(sb["pos"], n)
        norm_mod(n, modc(l, j, 1), modc(l, j, 0), xn, s_xn)
        qitems = [dict(cc=h % 4, last=(h % 4 == 3), gcol=DV_QG + 2 * l, rope=use_rope,
                       dst=qa[:, h, 0:n], slots=[s_qa[h]]) for h in range(NQH)]
        proj_heads(qitems, n)
        def emit_aux0():
            zc0 = sb["zcol"]
            lz, rz = sb["lzero"], sb["rzero"]
            a0 = zc0 - (0 if lz else 1)
            a1 = zc0 + n + (0 if rz else 1)
            d0 = 1 if lz else 0
            sc.dma("sp", zh[:, :, d0:d0 + (a1 - a0)], z_d[:, :, a0:a1].rearrange("c p n -> p c n"), key="zh",
                   reads=[sd("z")], writes=[s_zh])
            if lz:
                sc.op("dve", lambda e: e.memset(zh[:, :, 0:1], 0.0), writes=[s_zh])
            if rz:
                sc.op("dve", lambda e: e.memset(zh[:, :, n + 1:n + 2], 0.0), writes=[s_zh])
            for c in range(4):
                cw = V_CONV + 3 * c
                sc.op("dve", lambda e, c=c, cw=cw: e.tensor_scalar(
                    out=tmp[2][:, 0:n], in0=zh[:, c, 1:n + 1], scalar1=vcol(cw + 1), scalar2=None, op0=ALU.mult),
                    reads=[s_zh, s_vecs], writes=[s_tmp[2]])
                sc.op("dve", lambda e, c=c, cw=cw: e.scalar_tensor_tensor(
                    out=tmp[2][:, 0:n], in0=zh[:, c, 0:n], scalar=vcol(cw), in1=tmp[2][:, 0:n], op0=ALU.mult, op1=ALU.add),
                    reads=[s_zh, s_vecs, s_tmp[2]], writes=[s_tmp[2]])
                sc.op("dve", lambda e, c=c, cw=cw: e.scalar_tensor_tensor(
                    out=tmp[2][:, 0:n], in0=zh[:, c, 2:n + 2], scalar=vcol(cw + 2), in1=tmp[2][:, 0:n], op0=ALU.mult, op1=ALU.add),
                    reads=[s_zh, s_vecs, s_tmp[2]], writes=[s_tmp[2]])
                sc.dma("sp", tmp[3][:, 0:n], gb_d[c, :, zc0:zc0 + n], key="gbt", reads=[sd("gb")], writes=[s_tmp[3]])
                sc.op("dve", lambda e, c=c: e.tensor_tensor(out=qa[:, 12 + c, 0:n], in0=tmp[2][:, 0:n], in1=tmp[3][:, 0:n], op=ALU.mult),
                      reads=[s_tmp[2], s_tmp[3]], writes=[s_qa[12 + c]])

        for h in range(NQH):
            g = h // 3
            if h == 1 and l == 0:
                emit_aux0()
            if l == 0 and kind != "ctx":
                steps = []
                for half in range(2):
                    if half == 0:
                        i = kv_load(g, [(0, 0, 2176)], [(0, 0, 17)])
                    else:
                        i = kv_load(g, [(0, 2176, 2176)], [(0, 17, 17)])
                    steps.append((i, [(ktile(i, t), vtile(i, t), 0, n, None) for t in range(17)]))
                attn_head(l, h, n, steps)
            elif l == 0:
                i = kv_load(g, [(0, S, LC)], [(0, 32, 2)])
                attn_head(l, h, n, [(i, [(ktile(i, t), vtile(i, t), 0, n, None) for t in range(2)])])
            else:
                j0 = sb["pos"] // 128
                lo, hi = max(j0 - 1, 0), min(j0 + 4, 16)
                nb = hi - lo + 1
                i = kv_load(g, [(0, NLOC, LC), (LC, lo * 128, nb * 128)], [(0, NLOC // 128, 2), (2, lo, nb)])
                tiles = [(ktile(i, t), vtile(i, t), 0, n, None) for t in range(2)]
                for kt in range(lo, hi + 1):
                    qlo, qhi = max(kt - 1, j0), min(kt + 1, j0 + 3)
                    c0, c1 = (qlo - j0) * 128, (qhi - j0 + 1) * 128
                    m0 = (qlo - (kt - 1)) * 128
                    m = cb[:, C_M3 + m0:C_M3 + m0 + (c1 - c0)]
                    tiles.append((ktile(i, 2 + kt - lo), vtile(i, 2 + kt - lo), c0, c1, m))
                attn_head(l, h, n, [(i, tiles)])
        two = [add_w16(wout, 512 * t) for t in range(4)]
        if l == 0:
            pass
        else:
            j0 = sb["pos"] // 128
            lo, hi = max(j0 - 1, 0), min(j0 + 4, 16)
            nb = hi - lo + 1
            sc.dma("sp", ub[:, 0:nb, :], u_d[:, lo:hi + 1, :], key="ub", reads=[sd("u")], writes=[s_ub])
            for g in range(4):
                bi_ = 1 + (g % 2)

                def fn(e, g=g, bi_=bi_):
                    ins = None
                    for qt in range(4):
                        jt = j0 + qt
                        rels = [r for r in (-1, 0, 1) if lo <= jt + r <= hi]
                        for ii, r in enumerate(rels):
                            which = 3 if (jt == 0 and r == 0) else (r + 1)
                            bm = cb[:, C_BM + (g * 4 + which) * 128:C_BM + (g * 4 + which + 1) * 128]
                            ins = e.matmul(banks[bi_][:, qt * 128:(qt + 1) * 128],
                                           lhsT=ub[:, jt + r - lo, g * 128:(g + 1) * 128], rhs=bm,
                                           start=(ii == 0), stop=(ii == len(rels) - 1))
                    return ins
                sc.op("pe", fn, reads=[s_ub, s_cb], writes=[s_bank[bi_]])
                pi = g % 2
                sc.op("act", lambda e, bi_=bi_, pi=pi: e.activation(out=pls[pi][:, 0:n], in_=banks[bi_][:, 0:n], func=AF.Copy),
                      reads=[s_bank[bi_]], writes=[s_pls[pi]])
                sc.op("pe", lambda e, g=g, pi=pi: e.matmul(banks[3][:, 0:n], lhsT=cb[:, C_PW + g * 128:C_PW + (g + 1) * 128],
                                                          rhs=pls[pi][:, 0:n], start=True, stop=True),
                      reads=[s_pls[pi], s_cb], writes=[s_bank[3]])
                sc.op("dve", lambda e, g=g: e.tensor_scalar(out=qa[:, 12 + g, 0:n], in0=banks[3][:, 0:n],
                                                            scalar1=vcol(V_PSC + g), scalar2=None, op0=ALU.mult),
                      reads=[s_bank[3], s_vecs], writes=[s_qa[12 + g]])
        for t in range(4):
            ri = ws.take()
            wt = k16(ring[ri])
            for c in range(4):
                dch = t * 4 + c
                bi_ = 1 + (dch % 2)

                def fn(e, c=c, bi_=bi_, wt=wt):
                    ins = None
                    for k in range(KD):
                        ins = e.matmul(banks[bi_][:, 0:n], lhsT=wt[:, k, c * 128:(c + 1) * 128], rhs=qa[:, k, 0:n],
                                       start=(k == 0), stop=(k == KD - 1))
                    return ins
                sc.op("pe", fn, reads=[s_ring[ri]] + s_qa, writes=[s_bank[bi_]])
                if dch >= 1:
                    sq_chunk(dch - 1, n)
                sc.op("dve", lambda e, dch=dch, bi_=bi_: e.scalar_tensor_tensor(
                    out=Xb[:, dch, 0:n], in0=banks[bi_][:, 0:n], scalar=dcol(modc(l, j, 2, dch)), in1=Xb[:, dch, 0:n],
                    op0=ALU.mult, op1=ALU.add), reads=[s_bank[bi_], s_dv, s_Xb[dch]], writes=[s_Xb[dch]])
            ws.done()
        sq_chunk(KD - 1, n)
        store_Xb(hT_d[:, :, hcol:hcol + n].rearrange("k p n -> p k n"), n, sd("hT"))
        norm_finish(n, modc(l, j, 4), modc(l, j, 3), hn, s_hn, dst_c0=hn_c0)
    def stage2(l, hcol, ncols, gate_segs, mod_tiles=None):
        wg, wu, wd = W[l]["wg"], W[l]["wu"], W[l]["wd"]
        nsp = [(0, min(512, ncols))] + ([(512, ncols)] if ncols > 512 else [])
        for half in range(2):
            gu_ids = []
            for jj in range(11):
                f0 = (half * 22 + jj * 2) * 128
                gu_ids.append(ws.add([
                    (lambda r: r[:, 0:8192].rearrange("p (k n) -> p k n", k=16)[:, :, 0:256], wsrc16(wg, f0, 256)),
                    (lambda r: r[:, 0:8192].rearrange("p (k n) -> p k n", k=16)[:, :, 256:512], wsrc16(wu, f0, 256)),
                ]))
            for jj in range(11):
                ri = ws.take()
                wt = k16(ring[ri])
                for c2 in range(2):
                    fi = jj * 2 + c2
                    par = fi % 2
                    gb_, ub_ = (4 * par, 4 * par + 1), (4 * par + 2, 4 * par + 3)

                    def fn(e, c2=c2, wt=wt, bks=gb_, off=0):
                        ins = None
                        for si, (a, b) in enumerate(nsp):
                            for k in range(KD):
                                ins = e.matmul(banks[bks[si]][:, 0:b - a],
                                               lhsT=wt[:, k, off + c2 * 128:off + (c2 + 1) * 128],
                                               rhs=hn[:, k, a:b], start=(k == 0), stop=(k == KD - 1))
                        return ins
                    sc.op("pe", fn, reads=[s_ring[ri], s_hn], writes=[s_bank[b] for b in gb_[:len(nsp)]])
                    sc.op("pe", lambda e, fn=fn, ub_=ub_: fn(e, bks=ub_, off=256), reads=[s_ring[ri], s_hn],
                          writes=[s_bank[b] for b in ub_[:len(nsp)]])
                    for si, (a, b) in enumerate(nsp):
                        sl = 2 * par + si
                        sc.op("act", lambda e, si=si, a=a, b=b, sl=sl, gb_=gb_: e.activation(
                            out=sil[sl][:, 0:b - a], in_=banks[gb_[si]][:, 0:b - a], func=AF.Silu),
                            reads=[s_bank[gb_[si]]], writes=[s_sil[sl]])
                        sc.op("dve", lambda e, si=si, a=a, b=b, sl=sl, ub_=ub_, fi=fi: e.tensor_tensor(
                            out=aT[:, fi, a:b], in0=banks[ub_[si]][:, 0:b - a], in1=sil[sl][:, 0:b - a], op=ALU.mult),
                            reads=[s_bank[ub_[si]], s_sil[sl]], writes=[s_aT[fi]])
                ws.done()
            for _ in range(2):
                if mod_tiles:
                    p_, t_ = mod_tiles.pop(0)
                    emit_mod_tile(1, p_, t_, bank_i=0)
                    if t_ == 3:
                        emit_mod_fin(1, p_)
            for t in range(8):
                r0 = half * 22 * 128
                ws.add([(lambda r: r[:, 0:22 * 256].rearrange("p (f n) -> p f n", f=22),
                         wd[r0:r0 + 22 * 128, t * 256:(t + 1) * 256].rearrange("(f p) n -> p f n", p=128))])
            for t in range(8):
                ri = ws.take()
                wt = ring[ri][:, 0:22 * 256].rearrange("p (f n) -> p f n", f=22)
                for c2 in range(2):
                    dch = t * 2 + c2
                    par = dch % 2
                    yb = (4 * par, 4 * par + 1)
                    hi_ = dch % 3
                    sc.dma("sp", hb[hi_][:, 0:ncols], hT_d[dch, :, hcol:hcol + ncols], key=f"hb{hi_}",
                           reads=[sd("hT")], writes=[s_hb[hi_]])

                    def fn(e, c2=c2, wt=wt, yb=yb):
                        ins = None
                        for si, (a, b) in enumerate(nsp):
                            for f in range(22):
                                ins = e.matmul(banks[yb[si]][:, 0:b - a], lhsT=wt[:, f, c2 * 128:(c2 + 1) * 128],
                                               rhs=aT[:, f, a:b], start=(f == 0), stop=(f == 21))
                        return ins
                    sc.op("pe", fn, reads=[s_ring[ri]] + s_aT, writes=[s_bank[b] for b in yb[:len(nsp)]])
                    for (g0, g1, j) in gate_segs:
                        for si, (a, b) in enumerate(nsp):
                            x0, x1 = max(a, g0), min(b, g1)
                            if x1 <= x0:
                                continue
                            sc.op("dve", lambda e, si=si, a=a, x0=x0, x1=x1, j=j, yb=yb, hi_=hi_, dch=dch: e.scalar_tensor_tensor(
                                out=hb[hi_][:, x0:x1], in0=banks[yb[si]][:, x0 - a:x1 - a],
                                scalar=dcol(modc(l, j, 5, dch)), in1=hb[hi_][:, x0:x1], op0=ALU.mult, op1=ALU.add),
                                reads=[s_bank[yb[si]], s_dv, s_hb[hi_]], writes=[s_hb[hi_]])
                    sc.dma("sp", hT_d[dch, :, hcol:hcol + ncols], hb[hi_][:, 0:ncols], key=f"hb{hi_}",
                           reads=[s_hb[hi_]], writes=[sd("hT")])
                ws.done()
            for _ in range(2):
                if mod_tiles:
                    p_, t_ = mod_tiles.pop(0)
                    emit_mod_tile(1, p_, t_, bank_i=2)
                    if t_ == 3:
                        emit_mod_fin(1, p_)

    def stage3(hcol, n, ocol):
        load_Xb(hT_d[:, :, hcol:hcol + n].rearrange("k p n -> p k n"), n, [sd("hT")])
        for c in range(KD):
            sq_chunk(c, n)
        rstd_finish(n)
        for c in range(KD):
            sc.op("dve", lambda e, c=c: e.scalar_tensor_tensor(
                out=Xb[:, c, 0:n], in0=Xb[:, c, 0:n], scalar=dcol(DV_GF + c), in1=tmp[0][:, 0:n],
                op0=ALU.mult, op1=ALU.mult), reads=[s_Xb[c], s_dv, s_tmp[0]], writes=[s_Xb[c]])
        store_Xb(outT[:, ocol:ocol + n].rearrange("(k p) n -> p k n", p=128), n, sd("out"))

    def body():
        for p in range(6):
            emit_mod_part(0, p)
        blocks_a0 = [dict(n=512, scol=512 * i, kcol=512 * i, pos=512 * i, ctx=False, aux=(i <= 4), zcol=512 * i)
                     for i in range(8)]
        blocks_a0.append(dict(n=LC, scol=S, kcol=S, pos=0, ctx=True, aux=True, zcol=2560))
        phase_a(0, [blocks_a0[0:2], blocks_a0[2:4], blocks_a0[4:6], blocks_a0[6:8], blocks_a0[8:9]])
        sc.barrier()
        if stop == "a0":
            return
        subs0 = [dict(n=512, kind="lat", scol=512 * i, hcol=512 * i, pos=512 * i, zcol=512 * i,
                      lzero=(i == 0), rzero=False) for i in range(4)]
        subs0.append(dict(n=128, kind="lat", scol=2048, hcol=2048, pos=2048, zcol=2048, lzero=False, rzero=False))
        subs0.append(dict(n=LC, kind="ctx", scol=S, hcol=NLOC, pos=0, zcol=2560, lzero=True, rzero=True))
        for b in range(3):
            c0 = 0
            for si_, sb in enumerate(subs0[2 * b:2 * b + 2]):
                stage1(0, sb, c0, mod_part=2 * b + si_)
                c0 += sb["n"]
            sc.barrier()
            hcol = subs0[2 * b]["hcol"]
            segs = [(0, c0, 0)] if b < 2 else [(0, 128, 0), (128, 384, 1)]
            stage2(0, hcol, c0, segs, mod_tiles=[(2 * b + q_, t_) for q_ in range(2) for t_ in range(4)])
            sc.barrier()
        if stop == "l0":
            return
        blocks_a1 = [dict(n=512, scol=512 * i, kcol=512 * i, pos=512 * i, ctx=False, aux=True) for i in range(4)]
        blocks_a1.append(dict(n=128, scol=2048, kcol=2048, pos=2048, ctx=False, aux=True))
        blocks_a1.append(dict(n=LC, scol=NLOC, kcol=NLOC, pos=0, ctx=True, aux=False))
        phase_a(1, [blocks_a1[0:2], blocks_a1[2:4], blocks_a1[4:6]])
        sc.barrier()
        if stop == "a1":
            return
        subs1 = [dict(n=512, kind="lat", scol=512 * i, hcol=512 * i, pos=512 * i) for i in range(4)]
        for b in range(2):
            c0 = 0
            for sb in subs1[2 * b:2 * b + 2]:
                stage1(1, sb, c0)
                c0 += sb["n"]
            sc.barrier()
            stage2(1, 1024 * b, 1024, [(0, 1024, 0)])
            sc.barrier()
            for i in range(2):
                stage3(1024 * b + 512 * i, 512, 1024 * b + 512 * i)
            sc.barrier()

    sc.plan = True
    body()
    sc.plan = False
    kv_state["i"] = 0
    body()
    sc.barrier(engs=("sp",))
    sc.emit()
    return nc


_CACHE = {}


def _rope_tables(pos):
    pos = pos.astype(np.int64)
    row = (pos // 64).astype(np.float32)
    col = (pos % 64).astype(np.float32)
    inv = np.power(np.float32(10000.0), -np.arange(0, 64, 2, dtype=np.float32) / np.float32(64)).astype(np.float32)
    cosT = np.zeros((128, pos.shape[0]), np.float32)
    sinS = np.zeros((128, pos.shape[0]), np.float32)
    for p in range(128):
        ax = row if p < 64 else col
        ang = (ax * inv[p % 32]).astype(np.float32)
        cosT[p] = np.cos(ang)
        sgn = -1.0 if (p % 64) < 32 else 1.0
        sinS[p] = sgn * np.sin(ang)
    return np.stack([cosT, sinS], 0)


def _consts(mirror):
    cbf = np.zeros((128, NCB), np.float32)
    cbf[:, C_ONES:C_ONES + 128] = 1.0
    for m in range(128):
        k = m + 32 if (m % 64) < 32 else m - 32
        cbf[k, C_ROT + m] = 1.0
    a = np.arange(128)[:, None]
    b = np.arange(128)[None, :]
    cbf[:, C_M3:C_M3 + 128] = (a <= b)
    cbf[:, C_M3 + 128:C_M3 + 256] = 1.0
    cbf[:, C_M3 + 256:C_M3 + 384] = (a >= b)
    L = S
    for g, w in enumerate((2, 4, 8, 16)):
        Bf = np.zeros((384, 384), np.float64)
        for t in range(256):
            pos = (L - 1 - t) if mirror else t
            lo = min(max(pos - w // 2, 0), L)
            hi = min(max(pos - w // 2 + w, 0), L)
            for pp in range(lo, hi):
                tl = (L - 1 - pp) if mirror else pp
                if 0 <= tl < 384:
                    Bf[tl, t] += 1.0 / (hi - lo)
            Bf[t, t] -= 1.0
        base = C_BM + g * 4 * 128
        cbf[:, base + 0:base + 128] = Bf[0:128, 128:256]
        cbf[:, base + 128:base + 256] = Bf[128:256, 128:256]
        cbf[:, base + 256:base + 384] = Bf[256:384, 128:256]
        cbf[:, base + 384:base + 512] = Bf[0:128, 0:128]
    return cbf


def _fm(v, n):
    return np.ascontiguousarray(np.asarray(v, np.float32).reshape(n, 128).T)


def kernel(**inp):
    f32 = np.float32
    x = np.asarray(inp["x"], f32)
    ctx = np.asarray(inp["ctx"], f32)
    c = np.asarray(inp["c"], f32)
    c_ctx = np.asarray(inp["c_ctx"], f32)
    if "nc" not in _CACHE:
        _CACHE["nc"] = build_program()
    nc = _CACHE["nc"]
    wts = {}
    for l in range(2):
        for a, b in (("w_mod", "w_mod"), ("w_in", "w_in"), ("w_out", "w_out"), ("w_gate", "w_gate"),
                     ("w_up", "w_up"), ("w_down", "w_down")):
            wts[f"{a}{l}"] = np.ascontiguousarray(np.asarray(inp[f"l{l}_{b}"], f32))
    in_maps = []
    for core in range(8):
        b, half = core // 2, core % 2
        mirror = half == 1
        xs = x[b][::-1] if mirror else x[b]
        cs = ctx[b][::-1] if mirror else ctx[b]
        xTc = np.ascontiguousarray(np.concatenate([xs, cs], 0).T)
        pos = (S - 1 - np.arange(S)) if mirror else np.arange(S)
        vecs = np.zeros((128, NV), f32)
        for l in range(2):
            vecs[:, V_G1[l]:V_G1[l] + 16] = _fm(inp[f"l{l}_norm1_g"], 16)
            vecs[:, V_G2[l]:V_G2[l] + 16] = _fm(inp[f"l{l}_norm2_g"], 16)
            vecs[:, V_QG[l]] = np.asarray(inp[f"l{l}_q_norm_g"], f32)
            vecs[:, V_KG[l]] = np.asarray(inp[f"l{l}_k_norm_g"], f32)
            vecs[:, V_BM[l]:V_BM[l] + 96] = _fm(inp[f"l{l}_b_mod"], 96)
        vecs[:, V_GF:V_GF + 16] = _fm(inp["final_norm_g"], 16)
        cw = np.asarray(inp["l0_conv_w"], f32)
        if mirror:
            cw = cw[::-1]
        for ch in range(4):
            for t in range(3):
                vecs[:, V_CONV + 3 * ch + t] = cw[t, ch * 128:(ch + 1) * 128]
        vecs[:, V_PSC:V_PSC + 4] = _fm(inp["l1_pool_scale"], 4)
        vecs[:, V_SINK:V_SINK + 12] = np.asarray(inp["l1_sink"], f32)[None, :]
        cv = np.zeros((128, 16, 2), f32)
        cv[:, :, 0] = _fm(c[b], 16)
        cv[:, :, 1] = _fm(c_ctx, 16)
        vecs[:, V_CVEC:V_CVEC + 32] = cv.reshape(128, 32)
        cbf = _consts(mirror)
        pw = np.asarray(inp["l1_pool_w"], f32)
        cbf[:, C_PW:C_PW + 512] = np.transpose(pw, (1, 0, 2)).reshape(128, 512)
        m = {"xT": xTc, "rope": _rope_tables(pos), "vecs": vecs, "cbf": cbf}
        m.update(wts)
        in_maps.append(m)
    res = run_bass_kernel_spmd(nc, in_maps, core_ids=list(range(8)))
    out = np.zeros((4, S, D), f32)
    for core in range(8):
        b, half = core // 2, core % 2
        o = np.asarray(res.results[core]["outT"], f32).T
        if half == 1:
            out[b, 2048:] = o[::-1]
        else:
            out[b, :2048] = o
    _CACHE["last"] = res
    return out
```

```python
import numpy as np
import concourse.bass as bass
import concourse.mybir as mybir
from concourse.bass_utils import run_bass_kernel_spmd

F32 = mybir.dt.float32
BF16 = mybir.dt.bfloat16
AF = mybir.ActivationFunctionType
ALU = mybir.AluOpType

D = 2048
KD = 16
S = 4096
LC = 256
HD = 128
NQH = 12
NKV = 4
FF = 5632
NF = 44
EPS = 1e-6
ATT_SCALE = HD ** -0.5
NLOC = 2176
NH1 = NLOC + LC
ZC = 2560 + LC

V_G1 = [0, 16 + 16 + 2 + 96]
V_G2 = [16, V_G1[1] + 16]
V_QG = [32, V_G1[1] + 32]
V_KG = [33, V_G1[1] + 33]
V_BM = [34, V_G1[1] + 34]
_o = 2 * 130
V_GF = _o
V_CONV = _o + 16
V_PSC = _o + 28
V_SINK = _o + 32
V_CVEC = _o + 44
NV = _o + 76
C_ONES = 0
C_ROT = 128
C_M3 = 256
C_BM = 640
C_PW = C_BM + 2048
NCB = C_PW + 512


class Slot:
    __slots__ = ("name", "w", "rs", "const")

    def __init__(self, name):
        self.name = name
        self.w = None
        self.rs = {}
        self.const = False


class Sched:
    def __init__(self, nc):
        self.nc = nc
        self.q = {e: [] for e in ("pe", "act", "dve", "pool", "sp")}
        self.esem = {e: nc.alloc_semaphore("es_" + e) for e in ("pe", "act", "dve")}
        self.ecnt = {e: 0 for e in self.esem}
        self.waited = {e: {} for e in self.q}
        self.dsem = {}
        self.nops = 0
        self.plan = False

    def _wait(self, eng, tok):
        if tok is None:
            return
        sem, val = tok
        if eng == "pe" and sem is self.esem["pe"]:
            return
        k = id(sem)
        if self.waited[eng].get(k, 0) >= val:
            return
        self.waited[eng][k] = val
        self.q[eng].append(("wait", sem, val))

    def _deps(self, eng, reads, writes):
        for s in reads:
            self._wait(eng, s.w)
        for s in writes:
            self._wait(eng, s.w)
            for sem, val in s.rs.values():
                self._wait(eng, (sem, val))

    def _mark(self, tok, reads, writes):
        sem, val = tok
        for s in reads:
            if not s.const:
                s.rs[id(sem)] = (sem, val)
        for s in writes:
            s.w = tok
            s.rs = {}

    def op(self, eng, fn, reads=(), writes=()):
        if self.plan:
            return None
        self._deps(eng, reads, writes)
        self.ecnt[eng] += 1
        tok = (self.esem[eng], self.ecnt[eng])
        self.q[eng].append(("op", fn, self.esem[eng]))
        self._mark(tok, reads, writes)
        self.nops += 1
        return tok

    def dma(self, q, out, in_, key, reads=(), writes=(), chain=False):
        if self.plan:
            return None
        if key not in self.dsem:
            self.dsem[key] = [self.nc.alloc_semaphore("ds_" + key), 0]
        ent = self.dsem[key]
        if ent[1] > 0 and not chain:
            self._wait(q, (ent[0], ent[1]))
        self._deps(q, reads, writes)
        ent[1] += 16
        tok = (ent[0], ent[1])
        self.q[q].append(("dma", out, in_, ent[0]))
        self._mark(tok, reads, writes)
        return tok

    def barrier(self, engs=("pe", "act", "dve", "sp"), skip_keys=("ring0", "ring1", "ring2", "ring3")):
        if self.plan:
            return
        for e in engs:
            for o in ("pe", "act", "dve"):
                if self.ecnt[o] > 0:
                    self._wait(e, (self.esem[o], self.ecnt[o]))
            for k, ent in self.dsem.items():
                if k in skip_keys or ent[1] == 0:
                    continue
                self._wait(e, (ent[0], ent[1]))

    def emit(self):
        nc = self.nc
        qs = self.q

        def run(lst, e):
            for it in lst:
                if it[0] == "wait":
                    e.wait_ge(it[1], it[2])
                elif it[0] == "op":
                    it[1](e).then_inc(it[2], 1)
                else:
                    e.dma_start(out=it[1], in_=it[2]).then_inc(it[3], 16)

        with nc.Block() as block:
            @block.tensor
            def _(e):
                run(qs["pe"], e)

            @block.scalar
            def _(e):
                run(qs["act"], e)

            @block.vector
            def _(e):
                run(qs["dve"], e)

            @block.gpsimd
            def _(e):
                run(qs["pool"], e)

            @block.sync
            def _(e):
                run(qs["sp"], e)


def build_program(debug=False, stop=None):
    nc = bass.Bass("TRN2", target_bir_lowering=False)
    sc = Sched(nc)

    def din(name, shape, dt=F32):
        return nc.dram_tensor(name, shape, dt, kind="ExternalInput").ap()

    xT = din("xT", [D, S + LC])
    rope = din("rope", [2, 128, S])
    vecs_d = din("vecs", [128, NV])
    cbf_d = din("cbf", [128, NCB])
    W = []
    for l in range(2):
        W.append(dict(
            mod=din(f"w_mod{l}", [D, 6 * D]),
            win=din(f"w_in{l}", [D, 4096 if l == 0 else 3072]),
            wout=din(f"w_out{l}", [D, D]),
            wg=din(f"w_gate{l}", [D, FF]),
            wu=din(f"w_up{l}", [D, FF]),
            wd=din(f"w_down{l}", [FF, D]),
        ))
    outT = nc.dram_tensor("outT", [D, 2048], F32, kind="ExternalOutput").ap()
    skind = "ExternalOutput" if debug else "Internal"

    def dscr(name, shape, dt):
        return nc.dram_tensor(name, shape, dt, kind=skind).ap()

    kT_d = dscr("kT_d", [NKV, 128, S + LC], BF16)
    v_d = dscr("v_d", [NKV, 128, 34, 128], BF16)
    z_d = dscr("z_d", [4, 128, ZC], F32)
    gb_d = dscr("gb_d", [4, 128, ZC], F32)
    u_d = dscr("u_d", [128, 17, 512], BF16)
    hT_d = dscr("hT_d", [KD, 128, NH1], F32)

    ARENA = 53100
    arena = nc.alloc_sbuf_tensor("arena", [128, ARENA], F32)
    arena_ap = arena[:, :]
    ptr = [0]

    def carve(words, at=None):
        if at is None:
            at = ptr[0]
            ptr[0] += words
        assert at + words <= ARENA, (at, words)
        return arena[:, at:at + words]

    def bf(ap, **kw):
        v = ap.bitcast(BF16)
        if kw:
            v = v.rearrange("p (a b) -> p a b", **kw)
        return v

    def f3(ap, **kw):
        return ap.rearrange("p (a b) -> p a b", **kw)

    cb = bf(carve(NCB // 2))
    vecs = carve(NV)
    dv = carve(512)
    hn = bf(carve(KD * 1024 // 2), a=KD)
    NRING = 3
    ring = [bf(carve(4096)) for _ in range(NRING)]
    base = ptr[0]
    Xb = f3(carve(KD * 512), a=KD)
    xn = bf(carve(KD * 512 // 2), a=KD)
    qa = bf(carve(KD * 512 // 2), a=KD)
    kvr = [bf(carve(2176)) for _ in range(2)]
    Pb = [bf(carve(512)) for _ in range(3)]
    tmp = [carve(512) for _ in range(6)]
    cosb = carve(512)
    sinb = carve(512)
    sqb = [bf(carve(256)) for _ in range(2)]
    knb = [bf(carve(256)) for _ in range(2)]
    end1 = ptr[0]
    xnA = bf(arena[:, base + KD * 512:base + KD * 512 + KD * 512], a=KD)
    ptr[0] = base + KD * 512 + KD * 512
    gcS = f3(carve(4 * 1024), a=4)
    zbuf = [carve(512) for _ in range(3)]
    assert ptr[0] <= base + KD * 512 + KD * 512 + 2 * 2176 + 3 * 512
    ptr[0] = end1
    zh = f3(carve(4 * 516), a=4)
    end1b = ptr[0]
    Psum = [bf(carve(256)) for _ in range(2)]
    end1c = ptr[0]
    ptr[0] = end1
    vbuf = [bf(carve(256)) for _ in range(2)]
    kout = [bf(carve(256)) for _ in range(2)]
    cos2 = carve(512)
    sin2 = carve(512)
    assert ptr[0] <= end1b
    ptr[0] = end1
    ub = bf(carve(6 * 256), a=6)
    pls = [bf(carve(256)) for _ in range(2)]
    assert ptr[0] <= end1b
    ptr[0] = base
    aT = bf(carve(22 * 512), a=22)
    sil = [carve(512) for _ in range(4)]
    hb = [carve(1024) for _ in range(3)]
    assert ptr[0] <= ARENA and end1c <= ARENA, (ptr[0], end1c)

    pairs = [nc.alloc_psum_tensor(f"pair{i}", [128, 1024], F32) for i in range(4)]
    banks = [pairs[i // 2][:, (i % 2) * 512:(i % 2 + 1) * 512] for i in range(8)]

    def mk(n):
        return Slot(n)

    s_cb, s_vecs, s_dv, s_hn = mk("cb"), mk("vecs"), mk("dv"), mk("hn")
    s_ring = [mk(f"ring{i}") for i in range(NRING)]
    s_Xb, s_xn = [mk(f"Xb{c}") for c in range(16)], mk("xn")
    s_qa = [mk(f"qa{i}") for i in range(16)]
    s_kvr = [mk("kvr0"), mk("kvr1")]
    s_P = [mk(f"P{i}") for i in range(3)]
    s_Psum = [mk("Ps0"), mk("Ps1")]
    s_tmp = [mk(f"tmp{i}") for i in range(6)]
    s_cos, s_sin = mk("cos"), mk("sin")
    s_cos2, s_sin2 = mk("cos2"), mk("sin2")
    s_xnB = mk("xnB")
    s_sq = [mk("sq0"), mk("sq1")]
    s_knb = [mk("knb0"), mk("knb1")]
    s_vbuf = [mk("vb0"), mk("vb1")]
    s_kout = [mk("ko0"), mk("ko1")]
    s_gcS = mk("gcS")
    s_zbuf = [mk(f"zb{i}") for i in range(3)]
    s_zh, s_gbb, s_ub = mk("zh"), mk("gbb"), mk("ub")
    s_pls = [mk("pls0"), mk("pls1")]
    s_aT = [mk(f"aT{i}") for i in range(22)]
    s_sil = [mk(f"sil{i}") for i in range(4)]
    s_hb = [mk(f"hb{i}") for i in range(3)]
    s_bank = [mk(f"bank{i}") for i in range(8)]
    s_dram = {}

    def sd(name):
        if name not in s_dram:
            s_dram[name] = mk("d_" + name)
        return s_dram[name]

    ring_state = {"next": 0}

    def wload(parts):
        i = ring_state["next"] % NRING
        ring_state["next"] += 1
        first = True
        for dstf, src in parts:
            sc.dma("pool", dstf(ring[i]), src, key=f"ring{i}", writes=[s_ring[i]], chain=not first)
            first = False
        return i

    class WStream:
        def __init__(self):
            self.items = []
            self.loaded = 0
            self.used = 0
            self.slot_of = {}
            self.addi = 0

        def add(self, parts):
            if sc.plan:
                self.items.append(parts)
                return len(self.items) - 1
            self.addi += 1
            return self.addi - 1

        def pump(self):
            while self.loaded < len(self.items) and self.loaded < self.used + NRING:
                self.slot_of[self.loaded] = wload(self.items[self.loaded])
                self.loaded += 1

        def take(self):
            if sc.plan:
                return 0
            self.pump()
            assert self.used < self.loaded
            i = self.slot_of[self.used]
            self.used += 1
            return i

        def done(self):
            if not sc.plan:
                self.pump()

    ws = WStream()

    def k16(slot_ap, ncols=512):
        return slot_ap[:, 0:16 * ncols].rearrange("p (k n) -> p k n", k=16)

    def wsrc16(w, c0, ncols=512):
        return w[:, c0:c0 + ncols].rearrange("(k p) n -> p k n", p=128)

    def add_w16(w, c0, ncols=512):
        return ws.add([(lambda r, ncols=ncols: k16(r, ncols), wsrc16(w, c0, ncols))])

    ones = cb[:, C_ONES:C_ONES + 128]
    rotm = cb[:, C_ROT:C_ROT + 128]

    def vcol(c, n=1):
        return vecs[:, c:c + n]

    def dcol(c, n=1):
        return dv[:, c:c + n]

    DV_SCV = 0
    DV_MOD = 16
    DV_QG = 400
    DV_GF = 404
    DV_SINKE = 420
    DV_CD = 432
    DV_CH = 433

    def modc(l, j, m, c=0):
        return DV_MOD + (l * 2 + j) * 96 + m * 16 + c

    scv = bf(dv[:, DV_SCV:DV_SCV + 16])

    sc.dma("pool", cb[:, :], cbf_d[:, :], key="cb", writes=[s_cb])
    sc.dma("sp", vecs[:, :], vecs_d[:, :], key="vecs", writes=[s_vecs])
    s_cb.const = True
    sc.op("act", lambda e: e.activation(out=scv[:, :], in_=vcol(V_CVEC, 32), func=AF.Silu),
          reads=[s_vecs], writes=[s_dv])
    sc.op("act", lambda e: e.activation(out=dcol(DV_SINKE, 12), in_=vcol(V_SINK, 12), func=AF.Exp),
          reads=[s_vecs], writes=[s_dv])
    sc.op("dve", lambda e: e.memset(dcol(DV_CD), float(D * EPS)), writes=[s_dv])
    sc.op("dve", lambda e: e.memset(dcol(DV_CH), float(HD * EPS)), writes=[s_dv])
    for l in range(2):
        sc.op("dve", lambda e, l=l: e.tensor_scalar(out=dcol(DV_QG + 2 * l, 2), in0=vcol(V_QG[l], 2),
                                                    scalar1=float(np.sqrt(HD)), scalar2=None, op0=ALU.mult),
              reads=[s_vecs], writes=[s_dv])
    sc.op("dve", lambda e: e.tensor_scalar(out=dcol(DV_GF, 16), in0=vcol(V_GF, 16),
                                           scalar1=float(np.sqrt(D)), scalar2=None, op0=ALU.mult),
          reads=[s_vecs], writes=[s_dv])

    def emit_mod_part(l, p, bank_i=0):
        wm = W[l]["mod"]
        for t in range(4):
            add_w16(wm, (4 * p + t) * 512)
        mb = banks[bank_i]
        for t in range(4):
            ri = ws.take()
            wt = k16(ring[ri])

            def fn(e, t=t, wt=wt):
                ins = None
                for cc in range(4):
                    col = (t * 4 + cc) * 2
                    for k in range(16):
                        ins = e.matmul(mb[:, col:col + 2], lhsT=wt[:, k, cc * 128:(cc + 1) * 128],
                                       rhs=scv[:, 2 * k:2 * k + 2], start=(k == 0), stop=(k == 15))
                return ins
            sc.op("pe", fn, reads=[s_ring[ri], s_dv], writes=[s_bank[bank_i]])
            ws.done()
        mv = mb[:, 0:32].rearrange("p (c j) -> p c j", j=2)
        for j in range(2):
            sc.op("dve", lambda e, j=j: e.tensor_tensor(out=dcol(modc(l, j, p), 16), in0=mv[:, :, j],
                                                        in1=vcol(V_BM[l] + 16 * p, 16), op=ALU.add),
                  reads=[s_bank[bank_i], s_vecs], writes=[s_dv])
            if p in (1, 4):
                gc = V_G1[l] if p == 1 else V_G2[l]
                sc.op("dve", lambda e, j=j: e.tensor_scalar(
                    out=dcol(modc(l, j, p), 16), in0=dcol(modc(l, j, p), 16), scalar1=1.0,
                    scalar2=float(np.sqrt(D)), op0=ALU.add, op1=ALU.mult), reads=[s_dv], writes=[s_dv])
                sc.op("dve", lambda e, j=j, gc=gc: e.tensor_tensor(
                    out=dcol(modc(l, j, p), 16), in0=dcol(modc(l, j, p), 16), in1=vcol(gc, 16),
                    op=ALU.mult), reads=[s_dv, s_vecs], writes=[s_dv])

    def load_Xb(src3, n, extra_reads=()):
        for g4 in range(4):
            sc.dma("sp", Xb[:, 4 * g4:4 * g4 + 4, 0:n], src3[:, 4 * g4:4 * g4 + 4, :], key=f"Xb{g4}",
                   reads=list(extra_reads), writes=s_Xb[4 * g4:4 * g4 + 4])

    def store_Xb(dst3, n, dslot):
        for g4 in range(4):
            sc.dma("sp", dst3[:, 4 * g4:4 * g4 + 4, :], Xb[:, 4 * g4:4 * g4 + 4, 0:n], key=f"Xb{g4}",
                   reads=s_Xb[4 * g4:4 * g4 + 4], writes=[dslot])

    def load_rope(col0, n):
        sc.dma("sp", cosb[:, 0:n], rope[0, :, col0:col0 + n], key="cos", writes=[s_cos])
        sc.dma("sp", sinb[:, 0:n], rope[1, :, col0:col0 + n], key="sin", writes=[s_sin])

    def sq_chunk(c, n):
        sq_i = c % 2
        if sq_i == 0:
            sc.op("act", lambda e: e.activation(out=sqb[0][:, 0:n], in_=Xb[:, c, 0:n], func=AF.Square),
                  reads=[s_Xb[c]], writes=[s_sq[0]])
        else:
            sc.op("dve", lambda e: e.tensor_tensor(out=sqb[1][:, 0:n], in0=Xb[:, c, 0:n], in1=Xb[:, c, 0:n], op=ALU.mult),
                  reads=[s_Xb[c]], writes=[s_sq[1]])
        sc.op("pe", lambda e: e.matmul(banks[0][:, 0:n], lhsT=ones, rhs=sqb[sq_i][:, 0:n],
                                       start=(c == 0), stop=(c == KD - 1)),
              reads=[s_sq[sq_i], s_cb], writes=[s_bank[0]])

    def rstd_finish(n):
        sc.op("act", lambda e: e.activation(out=tmp[0][:, 0:n], in_=banks[0][:, 0:n], func=AF.Ln, bias=dcol(DV_CD), scale=1.0),
              reads=[s_bank[0], s_dv], writes=[s_tmp[0]])
        sc.op("act", lambda e: e.activation(out=tmp[0][:, 0:n], in_=tmp[0][:, 0:n], func=AF.Exp, scale=-0.5),
              reads=[s_tmp[0]], writes=[s_tmp[0]])

    def norm_finish(n, acol, bcol, dst, dst_slot, dst_c0=0):
        rstd_finish(n)
        for c in range(KD):
            ti = 1 + (c % 2)
            sc.op("dve", lambda e, c=c, ti=ti: e.scalar_tensor_tensor(
                out=tmp[ti][:, 0:n], in0=Xb[:, c, 0:n], scalar=dcol(acol + c), in1=tmp[0][:, 0:n],
                op0=ALU.mult, op1=ALU.mult), reads=[s_Xb[c], s_dv, s_tmp[0]], writes=[s_tmp[ti]])
            sc.op("act", lambda e, c=c, ti=ti: e.activation(
                out=dst[:, c, dst_c0:dst_c0 + n], in_=tmp[ti][:, 0:n], func=AF.Identity,
                bias=dcol(bcol + c), scale=1.0), reads=[s_tmp[ti], s_dv], writes=[dst_slot])

    def norm_mod(n, acol, bcol, dst, dst_slot, dst_c0=0):
        for c in range(KD):
            sq_chunk(c, n)
        norm_finish(n, acol, bcol, dst, dst_slot, dst_c0)

    def proj_fm(ri, cc, n, bank_i, src=None, src_slot=None, c0=0):
        src = xn if src is None else src
        src_slot = s_xn if src_slot is None else src_slot
        wt = k16(ring[ri])
        bk = banks[bank_i]

        def fn(e):
            ins = None
            for k in range(KD):
                ins = e.matmul(bk[:, 0:n], lhsT=wt[:, k, cc * 128:(cc + 1) * 128], rhs=src[:, k, c0:c0 + n],
                               start=(k == 0), stop=(k == KD - 1))
            return ins
        sc.op("pe", fn, reads=[s_ring[ri], src_slot], writes=[s_bank[bank_i]])

    def proj_heads(items, n_default=None):
        PB, SB, RB = (1, 2, 4), (0, 3), (5, 6)
        cur = {"ri": None}
        ni = len(items)

        def proj(i):
            it = items[i]
            if it.get("take", it["cc"] == 0):
                cur["ri"] = ws.take()
            proj_fm(cur["ri"], it["cc"], it.get("n", n_default), PB[i % 3], src=it.get("src"),
                    src_slot=it.get("src_slot"), c0=it.get("c0", 0))
            if it["last"]:
                ws.done()

        def A(i):
            it = items[i]
            n = it.get("n", n_default)
            p = i % 2
            pb, sb_ = PB[i % 3], SB[p]
            tr, tk = (3, 0)[p], (4, 1)[p]
            bk = banks[pb]
            sc.op("act", lambda e: e.activation(out=sqb[p][:, 0:n], in_=bk[:, 0:n], func=AF.Square),
                  reads=[s_bank[pb]], writes=[s_sq[p]])
            sc.op("pe", lambda e: e.matmul(banks[sb_][:, 0:n], lhsT=ones, rhs=sqb[p][:, 0:n], start=True, stop=True),
                  reads=[s_sq[p], s_cb], writes=[s_bank[sb_]])
            sc.op("act", lambda e: e.activation(out=tmp[tr][:, 0:n], in_=banks[sb_][:, 0:n], func=AF.Ln,
                                                bias=dcol(DV_CH), scale=1.0),
                  reads=[s_bank[sb_], s_dv], writes=[s_tmp[tr]])
            sc.op("act", lambda e: e.activation(out=tmp[tr][:, 0:n], in_=tmp[tr][:, 0:n], func=AF.Exp, scale=-0.5),
                  reads=[s_tmp[tr]], writes=[s_tmp[tr]])
            if not it["rope"]:
                sc.op("dve", lambda e: e.scalar_tensor_tensor(
                    out=it["dst"], in0=bk[:, 0:n], scalar=dcol(it["gcol"]), in1=tmp[tr][:, 0:n],
                    op0=ALU.mult, op1=ALU.mult), reads=[s_bank[pb], s_dv, s_tmp[tr]], writes=it["slots"])
                return
            sc.op("dve", lambda e: e.scalar_tensor_tensor(
                out=tmp[tk][:, 0:n], in0=bk[:, 0:n], scalar=dcol(it["gcol"]), in1=tmp[tr][:, 0:n],
                op0=ALU.mult, op1=ALU.mult), reads=[s_bank[pb], s_dv, s_tmp[tr]], writes=[s_tmp[tk]])
            sc.op("act", lambda e: e.activation(out=knb[p][:, 0:n], in_=tmp[tk][:, 0:n], func=AF.Copy),
                  reads=[s_tmp[tk]], writes=[s_knb[p]])

        def B(i):
            it = items[i]
            n = it.get("n", n_default)
            cosb_, sinb_, s_cos_, s_sin_ = it.get("tables", (cosb, sinb, s_cos, s_sin))
            p = i % 2
            tk, t5, rb = (4, 1)[p], (5, 2)[p], RB[p]
            if it["rope"]:
                sc.op("pe", lambda e: e.matmul(banks[rb][:, 0:n], lhsT=rotm, rhs=knb[p][:, 0:n], start=True, stop=True),
                      reads=[s_knb[p], s_cb], writes=[s_bank[rb]])
                sc.op("dve", lambda e: e.tensor_tensor(out=tmp[t5][:, 0:n], in0=banks[rb][:, 0:n], in1=sinb_[:, 0:n], op=ALU.mult),
                      reads=[s_bank[rb], s_sin_], writes=[s_tmp[t5]])
                sc.op("dve", lambda e: e.tensor_tensor(out=tmp[tk][:, 0:n], in0=tmp[tk][:, 0:n], in1=cosb_[:, 0:n], op=ALU.mult),
                      reads=[s_tmp[tk], s_cos_], writes=[s_tmp[tk]])
                sc.op("dve", lambda e: e.tensor_tensor(out=it["dst"], in0=tmp[tk][:, 0:n], in1=tmp[t5][:, 0:n], op=ALU.add),
                      reads=[s_tmp[tk], s_tmp[t5]], writes=it["slots"])
            if it.get("post"):
                it["post"]()

        proj(0)
        if ni > 1:
            proj(1)
        for i in range(ni):
            A(i)
            if i + 2 < ni:
                proj(i + 2)
            B(i)

    def phase_a(l, sblocks, mod_parts=()):
        win = W[l]["win"]
        mod_parts = list(mod_parts)

        def x_src(blk):
            n, scol = blk["n"], blk["scol"]
            return (xT[:, scol:scol + n].rearrange("(k p) n -> p k n", p=128) if l == 0
                    else hT_d[:, :, scol:scol + n].rearrange("k p n -> p k n"))
        prefetched = False
        for sbi, halves in enumerate(sblocks):
            nh = len(halves)
            xs = [s_xn, s_xnB]
            tabs = [(cosb, sinb, s_cos, s_sin), (cos2, sin2, s_cos2, s_sin2)]
            for hi, blk in enumerate(halves):
                n, scol = blk["n"], blk["scol"]
                jj = 1 if blk["ctx"] else 0
                if not (hi == 0 and prefetched):
                    load_Xb(x_src(blk), n, [sd("hT")] if l == 1 else [])
                if not blk["ctx"]:
                    cb_, sb_, sc_, ss_ = tabs[hi]
                    sc.dma("sp", cb_[:, 0:n], rope[0, :, blk["pos"]:blk["pos"] + n], key=f"cos{hi}", writes=[sc_])
                    sc.dma("sp", sb_[:, 0:n], rope[1, :, blk["pos"]:blk["pos"] + n], key=f"sin{hi}", writes=[ss_])
                norm_mod(n, modc(l, jj, 1), modc(l, jj, 0), xnA, xs[hi], dst_c0=512 * hi)
            prefetched = sbi + 1 < len(sblocks)
            if prefetched:
                nb0 = sblocks[sbi + 1][0]
                load_Xb(x_src(nb0), nb0["n"], [sd("hT")] if l == 1 else [])
            add_w16(win, 1536)
            kitems = []
            ix = 0
            for h in range(NKV):
                for hi, blk in enumerate(halves):
                    ko = ix % 2
                    ix += 1

                    def post(h=h, ko=ko, n=blk["n"], kcol=blk["kcol"]):
                        sc.dma("sp", kT_d[h, :, kcol:kcol + n], kout[ko][:, 0:n], key=f"ko{ko}",
                               reads=[s_kout[ko]], writes=[sd("kT")])
                    kitems.append(dict(cc=h, take=(h == 0 and hi == 0), last=(h == NKV - 1 and hi == nh - 1),
                                       gcol=DV_QG + 2 * l + 1, rope=not blk["ctx"], n=blk["n"], c0=512 * hi,
                                       src=xnA, src_slot=xs[hi], tables=tabs[hi],
                                       dst=kout[ko][:, 0:blk["n"]], slots=[s_kout[ko]], post=post))
            proj_heads(kitems)
            if mod_parts:
                emit_mod_part(l, mod_parts.pop(0), bank_i=7)

            def tok_major(wcol, which):
                add_w16(win, wcol)
                ri = ws.take()
                wt = k16(ring[ri])
                ix = 0
                for hi, blk in enumerate(halves):
                    if which == "u" and not blk["aux"]:
                        continue
                    for tt in range(blk["n"] // 128):
                        bi_ = 4 + (ix % 2)
                        vi = ix % 2
                        ix += 1
                        c0 = 512 * hi + tt * 128

                        def fn(e, c0=c0, bi_=bi_, wt=wt):
                            ins = None
                            for k in range(KD):
                                ins = e.matmul(banks[bi_][:, 0:512], lhsT=xnA[:, k, c0:c0 + 128],
                                               rhs=wt[:, k, :], start=(k == 0), stop=(k == KD - 1))
                            return ins
                        sc.op("pe", fn, reads=[s_ring[ri], xs[hi]], writes=[s_bank[bi_]])
                        sc.op("act", lambda e, bi_=bi_, vi=vi: e.activation(out=vbuf[vi][:, :], in_=banks[bi_][:, 0:512], func=AF.Copy),
                              reads=[s_bank[bi_]], writes=[s_vbuf[vi]])
                        if which == "v":
                            tg = blk["kcol"] // 128 + tt
                            sc.dma("sp", v_d[:, :, tg, :].rearrange("h p d -> p h d"),
                                   vbuf[vi][:, :].rearrange("p (h d) -> p h d", h=4), key=f"vb{vi}",
                                   reads=[s_vbuf[vi]], writes=[sd("v")])
                        else:
                            tg = blk["scol"] // 128 + tt
                            sc.dma("sp", u_d[:, tg, :], vbuf[vi][:, :], key=f"vb{vi}", reads=[s_vbuf[vi]], writes=[sd("u")])
                ws.done()
            tok_major(2048, "v")
            if l == 1 and any(b["aux"] for b in halves):
                tok_major(2560, "u")
            if l == 0 and any(b["aux"] for b in halves):
                for which, wcol in (("gb", 2560), ("gc", 3072), ("u", 3584)):
                    add_w16(win, wcol)
                    ri = ws.take()
                    ix = 0
                    for c in range(4):
                        for hi, blk in enumerate(halves):
                            if not blk["aux"]:
                                continue
                            n, zcol, off = blk["n"], blk["zcol"], 512 * hi
                            bi_ = 6 + (ix % 2)
                            zi = ix % 3
                            ix += 1
                            proj_fm(ri, c, n, bi_, src=xnA, src_slot=xs[hi], c0=off)
                            if which == "gc":
                                sc.op("act", lambda e, bi_=bi_, c=c, n=n, off=off: e.activation(
                                    out=gcS[:, c, off:off + n], in_=banks[bi_][:, 0:n], func=AF.Copy),
                                    reads=[s_bank[bi_]], writes=[s_gcS])
                                continue
                            if which == "gb":
                                sc.op("act", lambda e, bi_=bi_, zi=zi, n=n: e.activation(
                                    out=zbuf[zi][:, 0:n], in_=banks[bi_][:, 0:n], func=AF.Copy),
                                    reads=[s_bank[bi_]], writes=[s_zbuf[zi]])
                                dst_d, dsl = gb_d, sd("gb")
                            else:
                                sc.op("dve", lambda e, bi_=bi_, zi=zi, c=c, n=n, off=off: e.tensor_tensor(
                                    out=zbuf[zi][:, 0:n], in0=banks[bi_][:, 0:n], in1=gcS[:, c, off:off + n], op=ALU.mult),
                                    reads=[s_bank[bi_], s_gcS], writes=[s_zbuf[zi]])
                                dst_d, dsl = z_d, sd("z")
                            sc.dma("sp", dst_d[c, :, zcol:zcol + n], zbuf[zi][:, 0:n], key=f"zb{zi}",
                                   reads=[s_zbuf[zi]], writes=[dsl])
                    ws.done()

    def attn_head(l, h, n, key_steps):
        O_b, D_b = 6, 7
        NSB = 3
        flat = []
        for (ksl, tiles) in key_steps:
            for t in tiles:
                flat.append((ksl,) + t)
        groups = [flat[i:i + 2] for i in range(0, len(flat), 2)]
        ng = len(groups)
        nt = len(flat)

        def emit_S(gi):
            ss_ = gi % NSB
            for ti, (ksl, k_ap, v_ap, c0, c1, m) in enumerate(groups[gi]):
                bi_ = 2 * ss_ + ti
                sc.op("pe", lambda e, bi_=bi_, k_ap=k_ap: e.matmul(
                    banks[bi_][:, 0:n], lhsT=k_ap, rhs=qa[:, h, 0:n], start=True, stop=True),
                    reads=[s_kvr[ksl], s_qa[h]], writes=[s_bank[bi_]])

        for g0 in range(min(NSB, ng)):
            emit_S(g0)
        done = 0
        for gi in range(ng):
            grp = groups[gi]
            sp = gi % 3
            ss_ = gi % NSB
            P3 = Pb[sp].rearrange("p (b n) -> p b n", b=2)
            S3 = pairs[ss_][:, :].rearrange("p (b n) -> p b n", b=2)
            nb_ = len(grp)
            sc.op("act", lambda e, P3=P3, S3=S3, nb_=nb_: e.activation(
                out=P3[:, 0:nb_, 0:n], in_=S3[:, 0:nb_, 0:n], func=AF.Exp, scale=float(ATT_SCALE)),
                reads=[s_bank[2 * ss_ + t_] for t_ in range(nb_)], writes=[s_P[sp]])
            for ti, (ksl, k_ap, v_ap, c0, c1, m) in enumerate(grp):
                if m is not None:
                    sc.op("dve", lambda e, ti=ti, c0=c0, c1=c1, m=m, P3=P3: e.tensor_tensor(
                        out=P3[:, ti, c0:c1], in0=P3[:, ti, c0:c1], in1=m, op=ALU.mult),
                        reads=[s_P[sp], s_cb], writes=[s_P[sp]])
            pair_sum = (nb_ == 2 and all(t[5] is None and t[3] == 0 and t[4] == n for t in grp))
            pq = gi % 2
            if pair_sum:
                sc.op("dve", lambda e, P3=P3, pq=pq: e.tensor_tensor(out=Psum[pq][:, 0:n], in0=P3[:, 0, 0:n], in1=P3[:, 1, 0:n], op=ALU.add),
                      reads=[s_P[sp]], writes=[s_Psum[pq]])
            for ti, (ksl, k_ap, v_ap, c0, c1, m) in enumerate(grp):
                first, last = (done == 0), (done == nt - 1)
                done += 1
                sc.op("pe", lambda e, v_ap=v_ap, ti=ti, c0=c0, c1=c1, first=first, last=last, P3=P3: e.matmul(
                    banks[O_b][:, c0:c1], lhsT=v_ap, rhs=P3[:, ti, c0:c1], start=first, stop=last),
                    reads=[s_P[sp], s_kvr[ksl]], writes=[s_bank[O_b]])
                if not pair_sum:
                    sc.op("pe", lambda e, ti=ti, c0=c0, c1=c1, first=first, last=last, P3=P3: e.matmul(
                        banks[D_b][:, c0:c1], lhsT=ones, rhs=P3[:, ti, c0:c1], start=first, stop=last),
                        reads=[s_P[sp], s_cb], writes=[s_bank[D_b]])
            if pair_sum:
                dfirst, dlast = (done == 2), (done == nt)
                sc.op("pe", lambda e, pq=pq, dfirst=dfirst, dlast=dlast: e.matmul(
                    banks[D_b][:, 0:n], lhsT=ones, rhs=Psum[pq][:, 0:n], start=dfirst, stop=dlast),
                    reads=[s_Psum[pq], s_cb], writes=[s_bank[D_b]])
            if gi + NSB < ng:
                emit_S(gi + NSB)
        sc.op("dve", lambda e: e.tensor_copy(out=tmp[2][:, 0:n], in_=banks[O_b][:, 0:n]),
              reads=[s_bank[O_b]], writes=[s_tmp[2]])
        if l == 1:
            sc.op("act", lambda e: e.activation(out=tmp[1][:, 0:n], in_=banks[D_b][:, 0:n], func=AF.Ln,
                                                bias=dcol(DV_SINKE + h), scale=1.0),
                  reads=[s_bank[D_b], s_dv], writes=[s_tmp[1]])
        else:
            sc.op("act", lambda e: e.activation(out=tmp[1][:, 0:n], in_=banks[D_b][:, 0:n], func=AF.Ln),
                  reads=[s_bank[D_b]], writes=[s_tmp[1]])
        sc.op("act", lambda e: e.activation(out=tmp[1][:, 0:n], in_=tmp[1][:, 0:n], func=AF.Exp, scale=-1.0),
              reads=[s_tmp[1]], writes=[s_tmp[1]])
        sc.op("dve", lambda e: e.tensor_tensor(out=qa[:, h, 0:n], in0=tmp[2][:, 0:n], in1=tmp[1][:, 0:n], op=ALU.mult),
              reads=[s_tmp[2], s_tmp[1]], writes=[s_qa[h]])

    kv_state = {"i": 0}

    def kv_load(g, kranges, vranges):
        i = kv_state["i"] % 2
        kv_state["i"] += 1
        first = True
        for (dc, scol, ncol) in kranges:
            sc.dma("sp", kvr[i][:, dc:dc + ncol], kT_d[g, :, scol:scol + ncol], key=f"kvr{i}",
                   reads=[sd("kT")], writes=[s_kvr[i]], chain=not first)
            first = False
        for (dt_, st, ntl) in vranges:
            dstv = kvr[i][:, 2176 + dt_ * 128:2176 + (dt_ + ntl) * 128].rearrange("p (t d) -> p t d", d=128)
            sc.dma("sp", dstv, v_d[g, :, st:st + ntl, :], key=f"kvr{i}", reads=[sd("v")], writes=[s_kvr[i]], chain=True)
        return i

    def ktile(i, t):
        return kvr[i][:, t * 128:(t + 1) * 128]

    def vtile(i, t):
        return kvr[i][:, 2176 + t * 128:2176 + (t + 1) * 128]

    def stage1(l, sb, hn_c0, mod_part=None):
        n, kind = sb["n"], sb["kind"]
        j = 1 if kind == "ctx" else 0
        use_rope = kind != "ctx"
        win, wout = W[l]["win"], W[l]["wout"]
        tq = [add_w16(win, 512 * t) for t in range(3)]
        scol, hcol = sb["scol"], sb["hcol"]
        src = (xT[:, scol:scol + n].rearrange("(k p) n -> p k n", p=128) if l == 0
               else hT_d[:, :, scol:scol + n].rearrange("k p n -> p k n"))
        load_Xb(src, n, [sd("hT")] if l == 1 else [])
        if use_rope:
            load_rope(sb["pos"], n)
        norm_mod(n, modc(l, j, 1), modc(l, j, 0), xn, s_xn)
        qitems = [dict(cc=h % 4, last=(h % 4 == 3), gcol=DV_QG + 2 * l, rope=use_rope,
                       dst=qa[:, h, 0:n], slots=[s_qa[h]]) for h in range(NQH)]
        proj_heads(qitems, n)
        def emit_aux0():
            zc0 = sb["zcol"]
            lz, rz = sb["lzero"], sb["rzero"]
            a0 = zc0 - (0 if lz else 1)
            a1 = zc0 + n + (0 if rz else 1)
            d0 = 1 if lz else 0
            sc.dma("sp", zh[:, :, d0:d0 + (a1 - a0)], z_d[:, :, a0:a1].rearrange("c p n -> p c n"), key="zh",
                   reads=[sd("z")], writes=[s_zh])
            if lz:
                sc.op("dve", lambda e: e.memset(zh[:, :, 0:1], 0.0), writes=[s_zh])
            if rz:
                sc.op("dve", lambda e: e.memset(zh[:, :, n + 1:n + 2], 0.0), writes=[s_zh])
            for c in range(4):
                cw = V_CONV + 3 * c
                sc.op("dve", lambda e, c=c, cw=cw: e.tensor_scalar(
                    out=tmp[2][:, 0:n], in0=zh[:, c, 1:n + 1], scalar1=vcol(cw + 1), scalar2=None, op0=ALU.mult),
                    reads=[s_zh, s_vecs], writes=[s_tmp[2]])
                sc.op("dve", lambda e, c=c, cw=cw: e.scalar_tensor_tensor(
                    out=tmp[2][:, 0:n], in0=zh[:, c, 0:n], scalar=vcol(cw), in1=tmp[2][:, 0:n], op0=ALU.mult, op1=ALU.add),
                    reads=[s_zh, s_vecs, s_tmp[2]], writes=[s_tmp[2]])
                sc.op("dve", lambda e, c=c, cw=cw: e.scalar_tensor_tensor(
                    out=tmp[2][:, 0:n], in0=zh[:, c, 2:n + 2], scalar=vcol(cw + 2), in1=tmp[2][:, 0:n], op0=ALU.mult, op1=ALU.add),
                    reads=[s_zh, s_vecs, s_tmp[2]], writes=[s_tmp[2]])
                sc.dma("sp", tmp[3][:, 0:n], gb_d[c, :, zc0:zc0 + n], key="gbt", reads=[sd("gb")], writes=[s_tmp[3]])
                sc.op("dve", lambda e, c=c: e.tensor_tensor(out=qa[:, 12 + c, 0:n], in0=tmp[2][:, 0:n], in1=tmp[3][:, 0:n], op=ALU.mult),
                      reads=[s_tmp[2], s_tmp[3]], writes=[s_qa[12 + c]])

        for h in range(NQH):
            g = h // 3
            if h == 1 and l == 0:
                emit_aux0()
            if l == 0 and kind != "ctx":
                steps = []
                for half in range(2):
                    if half == 0:
                        i = kv_load(g, [(0, 0, 2176)], [(0, 0, 17)])
                    else:
                        i = kv_load(g, [(0, 2176, 2176)], [(0, 17, 17)])
                    steps.append((i, [(ktile(i, t), vtile(i, t), 0, n, None) for t in range(17)]))
                attn_head(l, h, n, steps)
            elif l == 0:
                i = kv_load(g, [(0, S, LC)], [(0, 32, 2)])
                attn_head(l, h, n, [(i, [(ktile(i, t), vtile(i, t), 0, n, None) for t in range(2)])])
            else:
                j0 = sb["pos"] // 128
                lo, hi = max(j0 - 1, 0), min(j0 + 4, 16)
                nb = hi - lo + 1
                i = kv_load(g, [(0, NLOC, LC), (LC, lo * 128, nb * 128)], [(0, NLOC // 128, 2), (2, lo, nb)])
                tiles = [(ktile(i, t), vtile(i, t), 0, n, None) for t in range(2)]
                for kt in range(lo, hi + 1):
                    qlo, qhi = max(kt - 1, j0), min(kt + 1, j0 + 3)
                    c0, c1 = (qlo - j0) * 128, (qhi - j0 + 1) * 128
                    m0 = (qlo - (kt - 1)) * 128
                    m = cb[:, C_M3 + m0:C_M3 + m0 + (c1 - c0)]
                    tiles.append((ktile(i, 2 + kt - lo), vtile(i, 2 + kt - lo), c0, c1, m))
                attn_head(l, h, n, [(i, tiles)])
        two = [add_w16(wout, 512 * t) for t in range(4)]
        if l == 0:
            pass
        else:
            j0 = sb["pos"] // 128
            lo, hi = max(j0 - 1, 0), min(j0 + 4, 16)
            nb = hi - lo + 1
            sc.dma("sp", ub[:, 0:nb, :], u_d[:, lo:hi + 1, :], key="ub", reads=[sd("u")], writes=[s_ub])
            for g in range(4):
                bi_ = 1 + (g % 2)

                def fn(e, g=g, bi_=bi_):
                    ins = None
                    for qt in range(4):
                        jt = j0 + qt
                        rels = [r for r in (-1, 0, 1) if lo <= jt + r <= hi]
                        for ii, r in enumerate(rels):
                            which = 3 if (jt == 0 and r == 0) else (r + 1)
                            bm = cb[:, C_BM + (g * 4 + which) * 128:C_BM + (g * 4 + which + 1) * 128]
                            ins = e.matmul(banks[bi_][:, qt * 128:(qt + 1) * 128],
                                           lhsT=ub[:, jt + r - lo, g * 128:(g + 1) * 128], rhs=bm,
                                           start=(ii == 0), stop=(ii == len(rels) - 1))
                    return ins
                sc.op("pe", fn, reads=[s_ub, s_cb], writes=[s_bank[bi_]])
                pi = g % 2
                sc.op("act", lambda e, bi_=bi_, pi=pi: e.activation(out=pls[pi][:, 0:n], in_=banks[bi_][:, 0:n], func=AF.Copy),
                      reads=[s_bank[bi_]], writes=[s_pls[pi]])
                sc.op("pe", lambda e, g=g, pi=pi: e.matmul(banks[3][:, 0:n], lhsT=cb[:, C_PW + g * 128:C_PW + (g + 1) * 128],
                                                          rhs=pls[pi][:, 0:n], start=True, stop=True),
                      reads=[s_pls[pi], s_cb], writes=[s_bank[3]])
                sc.op("dve", lambda e, g=g: e.tensor_scalar(out=qa[:, 12 + g, 0:n], in0=banks[3][:, 0:n],
                                                            scalar1=vcol(V_PSC + g), scalar2=None, op0=ALU.mult),
                      reads=[s_bank[3], s_vecs], writes=[s_qa[12 + g]])
        for t in range(4):
            ri = ws.take()
            wt = k16(ring[ri])
            for c in range(4):
                dch = t * 4 + c
                bi_ = 1 + (dch % 2)

                def fn(e, c=c, bi_=bi_, wt=wt):
                    ins = None
                    for k in range(KD):
                        ins = e.matmul(banks[bi_][:, 0:n], lhsT=wt[:, k, c * 128:(c + 1) * 128], rhs=qa[:, k, 0:n],
                                       start=(k == 0), stop=(k == KD - 1))
                    return ins
                sc.op("pe", fn, reads=[s_ring[ri]] + s_qa, writes=[s_bank[bi_]])
                if dch >= 1:
                    sq_chunk(dch - 1, n)
                sc.op("dve", lambda e, dch=dch, bi_=bi_: e.scalar_tensor_tensor(
                    out=Xb[:, dch, 0:n], in0=banks[bi_][:, 0:n], scalar=dcol(modc(l, j, 2, dch)), in1=Xb[:, dch, 0:n],
                    op0=ALU.mult, op1=ALU.add), reads=[s_bank[bi_], s_dv, s_Xb[dch]], writes=[s_Xb[dch]])
            ws.done()
        sq_chunk(KD - 1, n)
        store_Xb(hT_d[:, :, hcol:hcol + n].rearrange("k p n -> p k n"), n, sd("hT"))
        norm_finish(n, modc(l, j, 4), modc(l, j, 3), hn, s_hn, dst_c0=hn_c0)
        if mod_part is not None:
            emit_mod_part(1, mod_part, bank_i=3)

    def stage2(l, hcol, ncols, gate_segs):
        wg, wu, wd = W[l]["wg"], W[l]["wu"], W[l]["wd"]
        nsp = [(0, min(512, ncols))] + ([(512, ncols)] if ncols > 512 else [])
        for half in range(2):
            gu_ids = []
            for jj in range(11):
                f0 = (half * 22 + jj * 2) * 128
                gu_ids.append(ws.add([
                    (lambda r: r[:, 0:8192].rearrange("p (k n) -> p k n", k=16)[:, :, 0:256], wsrc16(wg, f0, 256)),
                    (lambda r: r[:, 0:8192].rearrange("p (k n) -> p k n", k=16)[:, :, 256:512], wsrc16(wu, f0, 256)),
                ]))
            for t in range(8):
                r0 = half * 22 * 128
                ws.add([(lambda r: r[:, 0:22 * 256].rearrange("p (f n) -> p f n", f=22),
                         wd[r0:r0 + 22 * 128, t * 256:(t + 1) * 256].rearrange("(f p) n -> p f n", p=128))])
            for jj in range(11):
                ri = ws.take()
                wt = k16(ring[ri])
                for c2 in range(2):
                    fi = jj * 2 + c2
                    par = fi % 2
                    gb_, ub_ = (4 * par, 4 * par + 1), (4 * par + 2, 4 * par + 3)

                    def fn(e, c2=c2, wt=wt, bks=gb_, off=0):
                        ins = None
                        for si, (a, b) in enumerate(nsp):
                            for k in range(KD):
                                ins = e.matmul(banks[bks[si]][:, 0:b - a],
                                               lhsT=wt[:, k, off + c2 * 128:off + (c2 + 1) * 128],
                                               rhs=hn[:, k, a:b], start=(k == 0), stop=(k == KD - 1))
                        return ins
                    sc.op("pe", fn, reads=[s_ring[ri], s_hn], writes=[s_bank[b] for b in gb_[:len(nsp)]])
                    sc.op("pe", lambda e, fn=fn, ub_=ub_: fn(e, bks=ub_, off=256), reads=[s_ring[ri], s_hn],
                          writes=[s_bank[b] for b in ub_[:len(nsp)]])
                    for si, (a, b) in enumerate(nsp):
                        sl = 2 * par + si
                        sc.op("act", lambda e, si=si, a=a, b=b, sl=sl, gb_=gb_: e.activation(
                            out=sil[sl][:, 0:b - a], in_=banks[gb_[si]][:, 0:b - a], func=AF.Silu),
                            reads=[s_bank[gb_[si]]], writes=[s_sil[sl]])
                        sc.op("dve", lambda e, si=si, a=a, b=b, sl=sl, ub_=ub_, fi=fi: e.tensor_tensor(
                            out=aT[:, fi, a:b], in0=banks[ub_[si]][:, 0:b - a], in1=sil[sl][:, 0:b - a], op=ALU.mult),
                            reads=[s_bank[ub_[si]], s_sil[sl]], writes=[s_aT[fi]])
                ws.done()
            for t in range(8):
                ri = ws.take()
                wt = ring[ri][:, 0:22 * 256].rearrange("p (f n) -> p f n", f=22)
                for c2 in range(2):
                    dch = t * 2 + c2
                    par = dch % 2
                    yb = (4 * par, 4 * par + 1)
                    hi_ = dch % 3
                    sc.dma("sp", hb[hi_][:, 0:ncols], hT_d[dch, :, hcol:hcol + ncols], key=f"hb{hi_}",
                           reads=[sd("hT")], writes=[s_hb[hi_]])

                    def fn(e, c2=c2, wt=wt, yb=yb):
                        ins = None
                        for si, (a, b) in enumerate(nsp):
                            for f in range(22):
                                ins = e.matmul(banks[yb[si]][:, 0:b - a], lhsT=wt[:, f, c2 * 128:(c2 + 1) * 128],
                                               rhs=aT[:, f, a:b], start=(f == 0), stop=(f == 21))
                        return ins
                    sc.op("pe", fn, reads=[s_ring[ri]] + s_aT, writes=[s_bank[b] for b in yb[:len(nsp)]])
                    for (g0, g1, j) in gate_segs:
                        for si, (a, b) in enumerate(nsp):
                            x0, x1 = max(a, g0), min(b, g1)
                            if x1 <= x0:
                                continue
                            sc.op("dve", lambda e, si=si, a=a, x0=x0, x1=x1, j=j, yb=yb, hi_=hi_, dch=dch: e.scalar_tensor_tensor(
                                out=hb[hi_][:, x0:x1], in0=banks[yb[si]][:, x0 - a:x1 - a],
                                scalar=dcol(modc(l, j, 5, dch)), in1=hb[hi_][:, x0:x1], op0=ALU.mult, op1=ALU.add),
                                reads=[s_bank[yb[si]], s_dv, s_hb[hi_]], writes=[s_hb[hi_]])
                    sc.dma("sp", hT_d[dch, :, hcol:hcol + ncols], hb[hi_][:, 0:ncols], key=f"hb{hi_}",
                           reads=[s_hb[hi_]], writes=[sd("hT")])
                ws.done()

    def stage3(hcol, n, ocol):
        load_Xb(hT_d[:, :, hcol:hcol + n].rearrange("k p n -> p k n"), n, [sd("hT")])
        for c in range(KD):
            sq_chunk(c, n)
        rstd_finish(n)
        for c in range(KD):
            sc.op("dve", lambda e, c=c: e.scalar_tensor_tensor(
                out=Xb[:, c, 0:n], in0=Xb[:, c, 0:n], scalar=dcol(DV_GF + c), in1=tmp[0][:, 0:n],
                op0=ALU.mult, op1=ALU.mult), reads=[s_Xb[c], s_dv, s_tmp[0]], writes=[s_Xb[c]])
        store_Xb(outT[:, ocol:ocol + n].rearrange("(k p) n -> p k n", p=128), n, sd("out"))

    def body():
        for p in range(2):
            emit_mod_part(0, p)
        blocks_a0 = [dict(n=512, scol=512 * i, kcol=512 * i, pos=512 * i, ctx=False, aux=(i <= 4), zcol=512 * i)
                     for i in range(8)]
        blocks_a0.append(dict(n=LC, scol=S, kcol=S, pos=0, ctx=True, aux=True, zcol=2560))
        phase_a(0, [blocks_a0[0:2], blocks_a0[2:4], blocks_a0[4:6], blocks_a0[6:8], blocks_a0[8:9]],
                mod_parts=(2, 3, 4, 5))
        sc.barrier()
        if stop == "a0":
            return
        subs0 = [dict(n=512, kind="lat", scol=512 * i, hcol=512 * i, pos=512 * i, zcol=512 * i,
                      lzero=(i == 0), rzero=False) for i in range(4)]
        subs0.append(dict(n=128, kind="lat", scol=2048, hcol=2048, pos=2048, zcol=2048, lzero=False, rzero=False))
        subs0.append(dict(n=LC, kind="ctx", scol=S, hcol=NLOC, pos=0, zcol=2560, lzero=True, rzero=True))
        for b in range(3):
            c0 = 0
            for si_, sb in enumerate(subs0[2 * b:2 * b + 2]):
                stage1(0, sb, c0, mod_part=2 * b + si_)
                c0 += sb["n"]
            sc.barrier()
            hcol = subs0[2 * b]["hcol"]
            segs = [(0, c0, 0)] if b < 2 else [(0, 128, 0), (128, 384, 1)]
            stage2(0, hcol, c0, segs)
            sc.barrier()
        if stop == "l0":
            return
        blocks_a1 = [dict(n=512, scol=512 * i, kcol=512 * i, pos=512 * i, ctx=False, aux=True) for i in range(4)]
        blocks_a1.append(dict(n=128, scol=2048, kcol=2048, pos=2048, ctx=False, aux=True))
        blocks_a1.append(dict(n=LC, scol=NLOC, kcol=NLOC, pos=0, ctx=True, aux=False))
        phase_a(1, [blocks_a1[0:2], blocks_a1[2:4], blocks_a1[4:6]])
        sc.barrier()
        if stop == "a1":
            return
        subs1 = [dict(n=512, kind="lat", scol=512 * i, hcol=512 * i, pos=512 * i) for i in range(4)]
        for b in range(2):
            c0 = 0
            for sb in subs1[2 * b:2 * b + 2]:
                stage1(1, sb, c0)
                c0 += sb["n"]
            sc.barrier()
            stage2(1, 1024 * b, 1024, [(0, 1024, 0)])
            sc.barrier()
            for i in range(2):
                stage3(1024 * b + 512 * i, 512, 1024 * b + 512 * i)
            sc.barrier()

    sc.plan = True
    body()
    sc.plan = False
    kv_state["i"] = 0
    body()
    sc.barrier(engs=("sp",))
    sc.emit()
    return nc


_CACHE = {}


def _rope_tables(pos):
    pos = pos.astype(np.int64)
    row = (pos // 64).astype(np.float32)
    col = (pos % 64).astype(np.float32)
    inv = np.power(np.float32(10000.0), -np.arange(0, 64, 2, dtype=np.float32) / np.float32(64)).astype(np.float32)
    cosT = np.zeros((128, pos.shape[0]), np.float32)
    sinS = np.zeros((128, pos.shape[0]), np.float32)
    for p in range(128):
        ax = row if p < 64 else col
        ang = (ax * inv[p % 32]).astype(np.float32)
        cosT[p] = np.cos(ang)
        sgn = -1.0 if (p % 64) < 32 else 1.0
        sinS[p] = sgn * np.sin(ang)
    return np.stack([cosT, sinS], 0)


def _consts(mirror):
    cbf = np.zeros((128, NCB), np.float32)
    cbf[:, C_ONES:C_ONES + 128] = 1.0
    for m in range(128):
        k = m + 32 if (m % 64) < 32 else m - 32
        cbf[k, C_ROT + m] = 1.0
    a = np.arange(128)[:, None]
    b = np.arange(128)[None, :]
    cbf[:, C_M3:C_M3 + 128] = (a <= b)
    cbf[:, C_M3 + 128:C_M3 + 256] = 1.0
    cbf[:, C_M3 + 256:C_M3 + 384] = (a >= b)
    L = S
    for g, w in enumerate((2, 4, 8, 16)):
        Bf = np.zeros((384, 384), np.float64)
        for t in range(256):
            pos = (L - 1 - t) if mirror else t
            lo = min(max(pos - w // 2, 0), L)
            hi = min(max(pos - w // 2 + w, 0), L)
            for pp in range(lo, hi):
                tl = (L - 1 - pp) if mirror else pp
                if 0 <= tl < 384:
                    Bf[tl, t] += 1.0 / (hi - lo)
            Bf[t, t] -= 1.0
        base = C_BM + g * 4 * 128
        cbf[:, base + 0:base + 128] = Bf[0:128, 128:256]
        cbf[:, base + 128:base + 256] = Bf[128:256, 128:256]
        cbf[:, base + 256:base + 384] = Bf[256:384, 128:256]
        cbf[:, base + 384:base + 512] = Bf[0:128, 0:128]
    return cbf


def _fm(v, n):
    return np.ascontiguousarray(np.asarray(v, np.float32).reshape(n, 128).T)


def kernel(**inp):
    f32 = np.float32
    x = np.asarray(inp["x"], f32)
    ctx = np.asarray(inp["ctx"], f32)
    c = np.asarray(inp["c"], f32)
    c_ctx = np.asarray(inp["c_ctx"], f32)
    if "nc" not in _CACHE:
        _CACHE["nc"] = build_program()
    nc = _CACHE["nc"]
    wts = {}
    for l in range(2):
        for a, b in (("w_mod", "w_mod"), ("w_in", "w_in"), ("w_out", "w_out"), ("w_gate", "w_gate"),
                     ("w_up", "w_up"), ("w_down", "w_down")):
            wts[f"{a}{l}"] = np.ascontiguousarray(np.asarray(inp[f"l{l}_{b}"], f32))
    in_maps = []
    for core in range(8):
        b, half = core // 2, core % 2
        mirror = half == 1
        xs = x[b][::-1] if mirror else x[b]
        cs = ctx[b][::-1] if mirror else ctx[b]
        xTc = np.ascontiguousarray(np.concatenate([xs, cs], 0).T)
        pos = (S - 1 - np.arange(S)) if mirror else np.arange(S)
        vecs = np.zeros((128, NV), f32)
        for l in range(2):
            vecs[:, V_G1[l]:V_G1[l] + 16] = _fm(inp[f"l{l}_norm1_g"], 16)
            vecs[:, V_G2[l]:V_G2[l] + 16] = _fm(inp[f"l{l}_norm2_g"], 16)
            vecs[:, V_QG[l]] = np.asarray(inp[f"l{l}_q_norm_g"], f32)
            vecs[:, V_KG[l]] = np.asarray(inp[f"l{l}_k_norm_g"], f32)
            vecs[:, V_BM[l]:V_BM[l] + 96] = _fm(inp[f"l{l}_b_mod"], 96)
        vecs[:, V_GF:V_GF + 16] = _fm(inp["final_norm_g"], 16)
        cw = np.asarray(inp["l0_conv_w"], f32)
        if mirror:
            cw = cw[::-1]
        for ch in range(4):
            for t in range(3):
                vecs[:, V_CONV + 3 * ch + t] = cw[t, ch * 128:(ch + 1) * 128]
        vecs[:, V_PSC:V_PSC + 4] = _fm(inp["l1_pool_scale"], 4)
        vecs[:, V_SINK:V_SINK + 12] = np.asarray(inp["l1_sink"], f32)[None, :]
        cv = np.zeros((128, 16, 2), f32)
        cv[:, :, 0] = _fm(c[b], 16)
        cv[:, :, 1] = _fm(c_ctx, 16)
        vecs[:, V_CVEC:V_CVEC + 32] = cv.reshape(128, 32)
        cbf = _consts(mirror)
        pw = np.asarray(inp["l1_pool_w"], f32)
        cbf[:, C_PW:C_PW + 512] = np.transpose(pw, (1, 0, 2)).reshape(128, 512)
        m = {"xT": xTc, "rope": _rope_tables(pos), "vecs": vecs, "cbf": cbf}
        m.update(wts)
        in_maps.append(m)
    res = run_bass_kernel_spmd(nc, in_maps, core_ids=list(range(8)))
    out = np.zeros((4, S, D), f32)
    for core in range(8):
        b, half = core // 2, core % 2
        o = np.asarray(res.results[core]["outT"], f32).T
        if half == 1:
            out[b, 2048:] = o[::-1]
        else:
            out[b, :2048] = o
    _CACHE["last"] = res
    return out
```
